# Optimizing a Trainium2 kernel written in Bass

```python
import math
import jax, jax.numpy as jnp
from jax import lax
import numpy as np

D_MODEL = 1024
BATCH = 4
SEQ = 4096
DEPTH = 2
DEC_BATCH = 16
DEC_SEQ = 4096
PAST_LEN = 128

POOL_WIDTH = 256
POOL_GROUPS = 4
POOL_GROUP_DIM = POOL_WIDTH // POOL_GROUPS
POOL_WINDOWS = (2, 4, 8, 16)
N_HEADS = 8
QK_NOPE = 64
QK_ROPE = 32
QK_DIM = QK_NOPE + QK_ROPE
V_HEAD = 64
Q_LORA = 256
KV_LORA = 128
ROPE_BASE = 10000.0
ATTN_WIDTH = N_HEADS * V_HEAD
Q_BLOCK = 128
CONV_WIDTH = 256
CONV_K = 31
FOURIER_WIDTH = 256
FOURIER_GROUPS = 4
FOURIER_GROUP_DIM = FOURIER_WIDTH // FOURIER_GROUPS
N_BRANCH = 4
NORM_EPS = 1e-6
LN_EPS = 1e-5
IN_SIZES = (POOL_WIDTH, Q_LORA, KV_LORA + QK_ROPE, 2 * CONV_WIDTH, FOURIER_WIDTH,
            POOL_WIDTH, ATTN_WIDTH, CONV_WIDTH, FOURIER_WIDTH, N_BRANCH * D_MODEL)
N_IN = POOL_WIDTH + Q_LORA + KV_LORA + QK_ROPE + 2 * CONV_WIDTH + FOURIER_WIDTH + POOL_WIDTH + ATTN_WIDTH + CONV_WIDTH + FOURIER_WIDTH + N_BRANCH * D_MODEL

kernel_name = "hybrid_parallel_gated_encoder"


def rms_norm(x, g):
    xf = x.astype(jnp.float32)
    y = xf * lax.rsqrt(jnp.mean(xf * xf, axis=-1, keepdims=True) + NORM_EPS)
    return (y * g.astype(jnp.float32)).astype(x.dtype)


def layer_norm(x, g, b):
    xf = x.astype(jnp.float32)
    mu = jnp.mean(xf, axis=-1, keepdims=True)
    var = jnp.mean(jnp.square(xf - mu), axis=-1, keepdims=True)
    y = (xf - mu) * lax.rsqrt(var + LN_EPS)
    return (y * g.astype(jnp.float32) + b.astype(jnp.float32)).astype(x.dtype)


def pool_mixer(u, pool_w, pool_scale):
    B, S, C = u.shape
    uf = u.astype(jnp.float32)
    cs = jnp.concatenate([jnp.zeros((B, 1, C), jnp.float32), jnp.cumsum(uf, axis=1)], axis=1)
    t = jnp.arange(S)
    outs = []
    for g, w in enumerate(POOL_WINDOWS):
        sl = slice(g * POOL_GROUP_DIM, (g + 1) * POOL_GROUP_DIM)
        lo = jnp.clip(t - w // 2, 0, S)
        hi = jnp.clip(t + w // 2, 0, S)
        csg = cs[..., sl]
        win_sum = jnp.take(csg, hi, axis=1) - jnp.take(csg, lo, axis=1)
        cnt = (hi - lo).astype(jnp.float32)[None, :, None]
        outs.append(win_sum / cnt - uf[..., sl])
    p = jnp.stack(outs, axis=2).astype(u.dtype)
    p = jnp.einsum('bsgc,gcd->bsgd', p, pool_w).reshape(B, S, C)
    return p * pool_scale


def rope_tables(S):
    inv_freq = 1.0 / (ROPE_BASE ** (jnp.arange(0, QK_ROPE, 2, dtype=jnp.float32) / QK_ROPE))
    ang = jnp.arange(S, dtype=jnp.float32)[:, None] * inv_freq[None, :]
    ang = jnp.concatenate([ang, ang], axis=-1)
    return jnp.cos(ang)[None, :, None, :], jnp.sin(ang)[None, :, None, :]


def apply_rope(x, cos, sin):
    xf = x.astype(jnp.float32)
    x1, x2 = xf[..., : QK_ROPE // 2], xf[..., QK_ROPE // 2:]
    rot = jnp.concatenate([-x2, x1], axis=-1)
    return (xf * cos + rot * sin).astype(x.dtype)


def mla(c_q, kv_a, q_norm_g, w_uq, kv_norm_g, w_ukv):
    B, S, _ = c_q.shape
    q = (rms_norm(c_q, q_norm_g) @ w_uq).reshape(B, S, N_HEADS, QK_DIM)
    q_nope, q_rope = q[..., :QK_NOPE], q[..., QK_NOPE:]
    c_kv, k_rope = kv_a[..., :KV_LORA], kv_a[..., KV_LORA:]
    kv = (rms_norm(c_kv, kv_norm_g) @ w_ukv).reshape(B, S, N_HEADS, QK_NOPE + V_HEAD)
    k_nope, v = kv[..., :QK_NOPE], kv[..., QK_NOPE:]
    cos, sin = rope_tables(S)
    q_rope = apply_rope(q_rope, cos, sin)
    k_rope = apply_rope(k_rope[:, :, None, :], cos, sin)
    q = jnp.concatenate([q_nope, q_rope], axis=-1)
    k = jnp.concatenate([k_nope, jnp.broadcast_to(k_rope, (B, S, N_HEADS, QK_ROPE))], axis=-1)
    scale = QK_DIM ** -0.5
    nb = S // Q_BLOCK
    qb = q.reshape(B, nb, Q_BLOCK, N_HEADS, QK_DIM).transpose(1, 0, 2, 3, 4)

    def attend(qblk):
        s = jnp.einsum('bqhd,bkhd->bhqk', qblk, k, preferred_element_type=jnp.float32) * scale
        p = jax.nn.softmax(s, axis=-1).astype(v.dtype)
        return jnp.einsum('bhqk,bkhd->bqhd', p, v)

    o = lax.map(attend, qb)
    return o.transpose(1, 0, 2, 3, 4).reshape(B, S, ATTN_WIDTH)


def conv_module(u, conv_w, conv_b, ln_g, ln_b, pw_w, pw_b):
    a, b = u[..., :CONV_WIDTH], u[..., CONV_WIDTH:]
    a = a * jax.nn.sigmoid(b)
    y = lax.conv_general_dilated(a, conv_w[:, None, :].astype(a.dtype), window_strides=(1,),
                                 padding=[(CONV_K // 2, CONV_K // 2)],
                                 dimension_numbers=('NWC', 'WIO', 'NWC'),
                                 feature_group_count=CONV_WIDTH) + conv_b
    y = jax.nn.silu(layer_norm(y, ln_g, ln_b))
    return y @ pw_w + pw_b


def fourier_mixer(u, fourier_w):
    B, S, _ = u.shape
    ug = u.astype(jnp.float32).reshape(B, S, FOURIER_GROUPS, FOURIER_GROUP_DIM)
    f = jnp.fft.fft2(ug, axes=(1, 3), norm='ortho').real
    f = f.reshape(B, S, FOURIER_WIDTH).astype(u.dtype)
    return f @ fourier_w


def encoder_layer(x, pre_g, post_g, w_in, gate_b, q_norm_g, w_uq, kv_norm_g, w_ukv,
                  pool_w, pool_scale, conv_w, conv_b, conv_ln_g, conv_ln_b, conv_pw_w, conv_pw_b,
                  fourier_w, w_up_pool, w_up_attn, w_up_conv, w_up_fourier, w_out):
    B, S, D = x.shape
    h = rms_norm(x, pre_g)
    z = h @ w_in
    points = []
    acc = 0
    for sz in IN_SIZES[:-1]:
        acc += sz
        points.append(acc)
    (u_pool, c_q, kv_a, u_conv, u_four,
     g_pool, g_attn, g_conv, g_four, z_merge) = jnp.split(z, points, axis=-1)
    y_pool = pool_mixer(u_pool, pool_w, pool_scale) * jax.nn.silu(g_pool)
    y_attn = mla(c_q, kv_a, q_norm_g, w_uq, kv_norm_g, w_ukv) * jax.nn.silu(g_attn)
    y_conv = conv_module(u_conv, conv_w, conv_b, conv_ln_g, conv_ln_b, conv_pw_w, conv_pw_b) * jax.nn.silu(g_conv)
    y_four = fourier_mixer(u_four, fourier_w) * jax.nn.silu(g_four)
    gates = jax.nn.sigmoid((z_merge + gate_b).reshape(B, S, N_BRANCH, D))
    m = (gates[:, :, 0] * (y_pool @ w_up_pool)
         + gates[:, :, 1] * (y_attn @ w_up_attn)
         + gates[:, :, 2] * (y_conv @ w_up_conv)
         + gates[:, :, 3] * (y_four @ w_up_fourier))
    out = m @ w_out
    return x + rms_norm(out, post_g)


def setup_inputs(seed: int = 0) -> dict:
    key = jax.random.key(seed)
    ks = jax.random.split(key, 32)
    L, D = DEPTH, D_MODEL
    f32 = jnp.float32

    def nrm(k, shape, fan_in):
        return jax.random.normal(k, shape, f32) * (fan_in ** -0.5)

    def gain(k, shape):
        return 1.0 + 0.02 * jax.random.normal(k, shape, f32)

    def small(k, shape):
        return 0.01 * jax.random.normal(k, shape, f32)

    return {
        "x_prompt": jax.random.normal(ks[0], (BATCH, SEQ, D), f32),
        "x_sample": jax.random.normal(ks[1], (DEC_BATCH, DEC_SEQ, D), f32),
        "pre_norm_g": gain(ks[2], (L, D)),
        "post_norm_g": gain(ks[3], (L, D)),
        "w_in": nrm(ks[4], (L, D, N_IN), D),
        "gate_b": small(ks[5], (L, N_BRANCH * D)),
        "q_norm_g": gain(ks[6], (L, Q_LORA)),
        "w_uq": nrm(ks[7], (L, Q_LORA, N_HEADS * QK_DIM), Q_LORA),
        "kv_norm_g": gain(ks[8], (L, KV_LORA)),
        "w_ukv": nrm(ks[9], (L, KV_LORA, N_HEADS * (QK_NOPE + V_HEAD)), KV_LORA),
        "pool_w": nrm(ks[10], (L, POOL_GROUPS, POOL_GROUP_DIM, POOL_GROUP_DIM), POOL_GROUP_DIM),
        "pool_scale": 1.0 + 0.1 * jax.random.normal(ks[11], (L, POOL_WIDTH), f32),
        "conv_w": nrm(ks[12], (L, CONV_K, CONV_WIDTH), CONV_K),
        "conv_b": small(ks[13], (L, CONV_WIDTH)),
        "conv_ln_g": gain(ks[14], (L, CONV_WIDTH)),
        "conv_ln_b": small(ks[15], (L, CONV_WIDTH)),
        "conv_pw_w": nrm(ks[16], (L, CONV_WIDTH, CONV_WIDTH), CONV_WIDTH),
        "conv_pw_b": small(ks[17], (L, CONV_WIDTH)),
        "fourier_w": nrm(ks[18], (L, FOURIER_WIDTH, FOURIER_WIDTH), FOURIER_WIDTH),
        "w_up_pool": nrm(ks[19], (L, POOL_WIDTH, D), POOL_WIDTH),
        "w_up_attn": nrm(ks[20], (L, ATTN_WIDTH, D), ATTN_WIDTH),
        "w_up_conv": nrm(ks[21], (L, CONV_WIDTH, D), CONV_WIDTH),
        "w_up_fourier": nrm(ks[22], (L, FOURIER_WIDTH, D), FOURIER_WIDTH),
        "w_out": nrm(ks[23], (L, D, D), D),
    }


def reference(x_prompt, x_sample, pre_norm_g, post_norm_g, w_in, gate_b, q_norm_g, w_uq, kv_norm_g, w_ukv,
              pool_w, pool_scale, conv_w, conv_b, conv_ln_g, conv_ln_b, conv_pw_w, conv_pw_b,
              fourier_w, w_up_pool, w_up_attn, w_up_conv, w_up_fourier, w_out):
    def trunk(x):
        for l in range(DEPTH):
            x = encoder_layer(x, pre_norm_g[l], post_norm_g[l], w_in[l], gate_b[l], q_norm_g[l], w_uq[l],
                              kv_norm_g[l], w_ukv[l], pool_w[l], pool_scale[l], conv_w[l], conv_b[l],
                              conv_ln_g[l], conv_ln_b[l], conv_pw_w[l], conv_pw_b[l], fourier_w[l],
                              w_up_pool[l], w_up_attn[l], w_up_conv[l], w_up_fourier[l], w_out[l])
        return x

    y_prompt = trunk(x_prompt)
    y_sample = trunk(x_sample)
    return (y_prompt, y_sample)
```

```python
import numpy as np
import ml_dtypes
from contextlib import ExitStack
import concourse.bass as bass
import concourse.mybir as mybir
from concourse.bass_utils import run_bass_kernel_spmd

F32 = mybir.dt.float32
BF16 = mybir.dt.bfloat16
AF = mybir.ActivationFunctionType
ALU = mybir.AluOpType
NPBF = ml_dtypes.bfloat16

D = 1024
NIN = 6816
DEPTH = 2
NCORES = 8
TT = 512
CONV_K = 31
OFF_UPOOL, OFF_CQ, OFF_KVA, OFF_UCONV, OFF_UFOUR = 0, 256, 512, 672, 1184
OFF_GPOOL, OFF_GATTN, OFF_GCONV, OFF_GFOUR, OFF_MERGE = 1440, 1696, 2208, 2464, 2720
SCALE = 96.0 ** -0.5

W_SIZES = [("WA", 128 * 8 * 448), ("WF", 128 * 8 * 256), ("WB", 6 * 128 * 8 * 128), ("WG", 10 * 128 * 8 * 128),
           ("WM", 32 * 128 * 8 * 128), ("WUP", 8 * 128 * 10 * 128), ("WOUT", 128 * 8 * 1024),
           ("WUQ", 128 * 2 * 768), ("WUQS", 128 * 2 * 768), ("WUKV", 128 * 1024), ("POOLW", 128 * 2 * 128),
           ("CONVPW", 128 * 2 * 256), ("FOURW", 128 * 2 * 256)]
W_OFF = {}
_o = 0
for _n, _s in W_SIZES:
    W_OFF[_n] = _o
    _o += _s
W_LAYER = ((_o + 2047) // 2048) * 2048
V_GPRE, V_GQ, V_GKV, V_GB, V_PSC, V_CW, V_CB, V_LNG, V_LNB, V_PWB = 0, 8, 10, 11, 43, 45, 107, 109, 111, 113
V_PINV = 116
NV = 118


class Chan:
    def __init__(self, sem):
        self.sem = sem
        self.cnt = 0
        self.last = None


class Op:
    __slots__ = ("eng", "fn", "deps", "is_dma", "chan", "val", "sig", "need")


class Prog:
    ENG = ("pe", "act", "dve", "pool", "sp")

    def __init__(self):
        self.q = {e: [] for e in self.ENG}
        self.lastw = {}
        self.rd = {}
        self.chans = []
        self.lastreal = {}

    def chan(self, sem):
        c = Chan(sem)
        self.chans.append(c)
        return c

    def op(self, eng, fn, r=(), w=(), chan=None):
        o = Op()
        o.eng, o.fn, o.is_dma, o.chan, o.sig, o.need, o.val = eng, fn, chan is not None, chan, 0, False, 0
        deps = {}

        def add(d, raw):
            if d is None:
                return
            if d.is_dma or o.is_dma or d.eng != eng or raw or eng != "pe":
                deps[id(d)] = d
        for k in r:
            add(self.lastw.get(k), True)
        for k in w:
            add(self.lastw.get(k), False)
            rr = self.rd.get(k)
            if rr:
                for d in rr.values():
                    add(d, False)
        if chan is not None:
            if chan.last is not None:
                deps[id(chan.last)] = chan.last
            chan.cnt += 16
            o.val = chan.cnt
            chan.last = o
        o.deps = list(deps.values())
        for k in r:
            rr = self.rd.get(k)
            if rr is None:
                rr = self.rd[k] = {}
            rr[("d", id(o)) if o.is_dma else eng] = o
        for k in w:
            self.lastw[k] = o
            self.rd[k] = {}
        self.q[eng].append(o)
        if not o.is_dma:
            self.lastreal[eng] = o
        return o

    def barrier(self):
        deps = [d for d in self.lastreal.values()] + [c.last for c in self.chans if c.last is not None]
        for e in self.ENG:
            o = Op()
            o.eng, o.fn, o.is_dma, o.chan, o.sig, o.need, o.val = e, None, False, None, 0, False, 0
            o.deps = [d for d in deps if d.is_dma or d.eng != e]
            self.q[e].append(o)
        self.lastw.clear()
        self.rd.clear()

    def finalize(self):
        for e in self.ENG:
            for o in self.q[e]:
                for d in o.deps:
                    d.need = True
        for e in self.ENG:
            c = 0
            for o in self.q[e]:
                if o.is_dma or o.fn is None:
                    continue
                if o.need:
                    c += 1
                    o.sig = c

    def replay(self, ename, e, sems):
        waited = {}
        for o in self.q[ename]:
            need = {}
            for d in o.deps:
                if d.is_dma:
                    sem, v = d.chan.sem, d.val
                else:
                    sem, v = sems[d.eng], d.sig
                k = id(sem)
                if waited.get(k, 0) >= v:
                    continue
                if k not in need or need[k][1] < v:
                    need[k] = (sem, v)
            for k, (sem, v) in need.items():
                e.wait_ge(sem, v)
                waited[k] = v
            if o.fn is None:
                continue
            ins = o.fn(e)
            if o.is_dma:
                ins.then_inc(o.chan.sem, 16)
            elif o.need:
                ins.then_inc(sems[o.eng], 1)


class Arena:
    def __init__(self, t, nbytes):
        self.t = t
        self.n = nbytes
        self.o = 0

    def alloc(self, nbytes, dtype=BF16):
        req = nbytes
        nbytes = (nbytes + 31) // 32 * 32
        assert self.o + nbytes <= self.n, f"arena overflow {self.o}+{nbytes}>{self.n}"
        ap = self.t[:, self.o // 2:(self.o + req) // 2]
        self.o += nbytes
        if dtype == F32:
            ap = ap.bitcast(F32)
        return ap

    def mark(self):
        return self.o

    def reset(self, m):
        self.o = m


def build(S, NSLOT, depth=DEPTH, dbg=False):
    NT = S // TT
    NS = S // 128
    nc = bass.Bass("TRN2", target_bir_lowering=False)
    P = Prog()
    es = ExitStack()
    xin = nc.dram_tensor("xin", [NSLOT * S, D], F32, kind="ExternalInput").ap()
    wpack = nc.dram_tensor("wpack", [DEPTH * W_LAYER], F32, kind="ExternalInput").ap()
    vecs = nc.dram_tensor("vecs", [128, DEPTH * NV], F32, kind="ExternalInput").ap()
    gpost_d = nc.dram_tensor("gpost", [DEPTH, D], F32, kind="ExternalInput").ap()
    ropetok_d = nc.dram_tensor("ropetok", [128, 2 * NS * 32], F32, kind="ExternalInput").ap()
    ropeT_d = nc.dram_tensor("ropeT", [2, 32, S], F32, kind="ExternalInput").ap()
    dft_d = nc.dram_tensor("dft", [2, S // TT, S // 512, 128, 4 * TT], BF16, kind="ExternalInput").ap()
    cbd_d = nc.dram_tensor("cbd", [128, 4 * 128], BF16, kind="ExternalInput").ap()
    pinv_d = nc.dram_tensor("pinv", [128, 2, S], F32, kind="ExternalInput").ap()
    yout = nc.dram_tensor("yout", [NSLOT * S, D], F32, kind="ExternalOutput").ap()
    wbf = nc.dram_tensor("wbf", [DEPTH * W_LAYER], BF16, kind="Internal").ap()
    x1 = nc.dram_tensor("x1", [NSLOT * S, D], F32, kind="Internal").ap()
    sk = "ExternalOutput" if dbg else "Internal"
    upool_d = nc.dram_tensor("upool_s", [2, 128, S], BF16, kind=sk).ap()
    aconv_d = nc.dram_tensor("aconv_s", [2, 128, S], BF16, kind=sk).ap()
    obr_d = nc.dram_tensor("obr_s", [10, 128, S], BF16, kind=sk).ap()

    ARENA_BYTES = 188 * 1024
    arena_t = es.enter_context(nc.sbuf_tensor("arena", [128, ARENA_BYTES // 2], BF16))
    psum_t = es.enter_context(nc.psum_tensor("psum", [128, 4096], F32))
    A = Arena(arena_t, ARENA_BYTES)
    bank = [psum_t[:, b * 512:(b + 1) * 512] for b in range(8)]
    sems = {e: es.enter_context(nc.semaphore("s_" + e)) for e in ("pe", "act", "dve", "pool")}
    nchan = [0]

    def newchan_raw():
        nchan[0] += 1
        return P.chan(es.enter_context(nc.semaphore("ch%d" % nchan[0])))
    chpool = [newchan_raw() for _ in range(20)]
    chidx = [0]

    def newchan():
        c = chpool[chidx[0] % len(chpool)]
        chidx[0] += 1
        return c

    def chan_phase():
        chidx[0] = 0

    def mm(out, lhsT, rhs, st, sp, r, w):
        P.op("pe", lambda e: e.matmul(out, lhsT=lhsT, rhs=rhs, start=st, stop=sp), r=r, w=w)

    def tr(out, in_, r, w):
        P.op("pe", lambda e: e.transpose(out=out, in_=in_, identity=ident), r=list(r) + ["ident"], w=w)

    def act(out, in_, func, r, w, scale=1.0, bias=None, accum=None):
        kw = {}
        if bias is not None:
            kw["bias"] = bias
        if accum is not None:
            kw["accum_out"] = accum
        P.op("act", lambda e: e.activation(out=out, in_=in_, func=func, scale=scale, **kw), r=r, w=w)

    def tt(eng, out, a, b, op, r, w):
        P.op(eng, lambda e: e.tensor_tensor(out=out, in0=a, in1=b, op=op), r=r, w=w)

    def ts(eng, out, a, s1, op0, r, w, s2=None, op1=None):
        if op1 is None:
            P.op(eng, lambda e: e.tensor_scalar(out=out, in0=a, scalar1=s1, scalar2=None, op0=op0), r=r, w=w)
        else:
            P.op(eng, lambda e: e.tensor_scalar(out=out, in0=a, scalar1=s1, scalar2=s2, op0=op0, op1=op1), r=r, w=w)

    def stt(out, in0, scalar, in1, op0, op1, r, w):
        P.op("dve", lambda e: e.scalar_tensor_tensor(out=out, in0=in0, scalar=scalar, in1=in1, op0=op0, op1=op1),
             r=r, w=w)

    def cp(eng, out, in_, r, w):
        if eng == "act":
            act(out, in_, AF.Copy, r, w)
        else:
            P.op(eng, lambda e: e.tensor_copy(out=out, in_=in_), r=r, w=w)

    def dma(out, in_, chan, r, w, eng="sp"):
        P.op(eng, lambda e: e.dma_start(out=out, in_=in_), r=r, w=w, chan=chan)

    def memset(eng, ap, val, w):
        P.op(eng, lambda e: e.memset(ap, val), r=(), w=w)

    ident = A.alloc(256)
    ones = A.alloc(256)
    cbd = A.alloc(4 * 256)
    c64 = cbd[:, 0:128]
    ns64 = cbd[:, 128:256]
    vec = A.alloc(DEPTH * NV * 4, F32)
    vder = A.alloc(DEPTH * NV * 4, F32)
    ropetok = A.alloc(2 * NS * 32 * 4, F32)
    mhalf = A.alloc(TT * 4, F32)
    gpost = A.alloc(D * 4, F32)
    small = A.alloc(96 * 4, F32)
    epsc = A.alloc(32, F32)
    c_const = newchan_raw()
    dma(ident, cbd_d[:, 384:512], c_const, [], ["ident"])
    dma(ones, cbd_d[:, 256:384], c_const, [], ["ones"])
    dma(cbd[:, 0:256], cbd_d[:, 0:256], c_const, [], ["cbd"])
    dma(vec, vecs[:, :], c_const, [], ["vec"])
    dma(ropetok, ropetok_d[:, :], c_const, [], ["ropetok"])
    memset("pool", mhalf, -0.5, ["mhalf"])
    memset("pool", epsc, 1e-5, ["epsc"])
    for l in range(DEPTH):
        b = l * NV
        for (o0, o1, f) in ((V_GB, V_GB + 32, 0.5), (V_CW, V_CW + 62, 0.5),
                            (V_LNG, V_LNG + 2, 0.5), (V_LNB, V_LNB + 2, 0.5)):
            ts("pool", vder[:, b + o0:b + o1], vec[:, b + o0:b + o1], f, ALU.mult, ["vec"], ["vder"])

    c_cast = newchan_raw()
    ROWS = DEPTH * W_LAYER // 2048
    wsrc = wpack.rearrange("(r c) -> r c", c=2048)
    wdst = wbf.rearrange("(r c) -> r c", c=2048)
    r0 = 0
    while r0 < ROWS:
        r1 = min(ROWS, r0 + 2048)
        dma(wdst[r0:r1, :], wsrc[r0:r1, :], c_cast, [], ["wbf"], eng="pool")
        r0 = r1
    P.barrier()
    base_mark = A.mark()

    def wview(l, name, pattern, **kw):
        o = l * W_LAYER + W_OFF[name]
        n = dict(W_SIZES)[name]
        return wbf[o:o + n].rearrange(pattern, **kw)

    def make_hT_a(l, src, row0, xb, xn, cx, xkeys, nkeys, so):
        for sub in range(4):
            kx = xkeys[sub]
            dma(xb[sub], src[row0 + sub * 128: row0 + (sub + 1) * 128, :], cx[sub], [], [kx])
            ssq = small[:, so + sub:so + sub + 1]
            act(xn[sub], xb[sub], AF.Square, [kx], [nkeys[sub], ("ssq", so, sub)], accum=ssq)
            ts("pool", small[:, so + 4 + sub:so + 5 + sub], ssq, 1.0 / D, ALU.mult, [("ssq", so, sub)],
               [("ssq2", so, sub)], s2=1e-6, op1=ALU.add)
            tt("pool", small[:, so + 8 + sub:so + 9 + sub], small[:, so + 4 + sub:so + 5 + sub], mhalf[:, 0:1],
               ALU.pow, [("ssq2", so, sub), "mhalf"], [("rstd", so, sub)])
            act(xn[sub], xb[sub], AF.Copy, [kx, ("rstd", so, sub)], [nkeys[sub]],
                scale=small[:, so + 8 + sub:so + 9 + sub])

    def make_hT_b(l, xn, nkeys, hT, pstb):
        vb = l * NV
        for sub in range(4):
            pt, ptk = pstb[sub % 2]
            for k in range(8):
                tr(pt[:, k * 128:(k + 1) * 128], xn[sub][:, k * 128:(k + 1) * 128], [nkeys[sub]], [ptk])
            tt("dve", hT[:, :, sub * 128:(sub + 1) * 128], pt.rearrange("p (k t) -> p k t", k=8),
               vec[:, vb + V_GPRE:vb + V_GPRE + 8].unsqueeze(2).broadcast_to([128, 8, 128]), ALU.mult,
               [ptk, "vec"], [("hT", id(hT))])

    for slot in range(NSLOT):
        for l in range(depth):
            src = xin if l == 0 else x1
            dst = yout if l == depth - 1 else x1
            R0 = slot * S
            vb = l * NV
            chan_phase()
            A.reset(base_mark)
            cqnT = A.alloc(2 * S * 2).rearrange("p (c t) -> p c t", c=2)
            ckvnT = A.alloc(S * 2)
            kropeT = A.alloc(S * 2)
            pA_mark = A.mark()
            ufour = A.alloc(NS * 256 * 2).rearrange("p (s c) -> p s c", c=256)
            p1_mark = A.mark()
            wA = A.alloc(8 * 448 * 2).rearrange("p (k n) -> p k n", k=8)
            wF = A.alloc(8 * 256 * 2).rearrange("p (k n) -> p k n", k=8)
            wB = A.alloc(6 * 8 * 128 * 2).rearrange("p (c k n) -> p c k n", c=6, k=8)
            xb2 = [A.alloc(D * 4, F32) for _ in range(2)]
            xn = [A.alloc(D * 2) for _ in range(4)]
            xb = [xb2[i % 2] for i in range(4)]
            xkeys = [("xb", i % 2) for i in range(4)]
            nkeys = [("xn", i) for i in range(4)]
            hTs = [A.alloc(8 * TT * 2).rearrange("p (k t) -> p k t", k=8) for _ in range(2)]
            tnh = [A.alloc(TT * 4, F32) for _ in range(4)]
            stg = [A.alloc(2 * TT * 2).rearrange("p (c t) -> p c t", c=2) for _ in range(2)]
            tokst = [A.alloc(512 * 2) for _ in range(2)]
            rtmp = [A.alloc(64 * 4, F32) for _ in range(2)]
            pstb = [(bank[6].bitcast(BF16), ("bk", 6)), (bank[7].bitcast(BF16), ("bk", 7))]
            cw = newchan()
            cx = [newchan() for _ in range(2)]
            cx = [cx[i % 2] for i in range(4)]
            cst = [newchan() for _ in range(2)]
            dma(wA, wview(l, "WA", "(p k n) -> p k n", p=128, k=8), cw, ["wbf"], ["wA"])
            dma(wF, wview(l, "WF", "(p k n) -> p k n", p=128, k=8), cw, ["wbf"], ["wF"])
            dma(wB, wview(l, "WB", "(c p k n) -> p c k n", c=6, p=128, k=8), cw, ["wbf"], ["wB"])
            dma(gpost, gpost_d[l:l + 1, :].partition_broadcast(128), cw, [], ["gpost"])

            for tt_i in range(NT):
                hT = hTs[tt_i % 2]
                hk = ("hT", id(hT))
                make_hT_a(l, src, R0 + tt_i * TT, xb, xn, cx, xkeys, nkeys, 0)
                make_hT_b(l, xn, nkeys, hT, pstb)
                t0 = tt_i * TT
                def fm_pool(c, tt_i=tt_i, t0=t0, hT=hT, hk=hk):
                    st = stg[0]
                    ps = bank[3]
                    for k in range(8):
                        mm(ps, wB[:, c, k, :], hT[:, k, :], k == 0, k == 7, [hk, "wB"], [("bk", 3)])
                    cp("act", st[:, c, :], ps, [("bk", 3)], [("stg", 0)])
                    if c == 1:
                        dma(upool_d[:, :, t0:t0 + TT].rearrange("c p t -> p c t"), st, cst[0], [("stg", 0)],
                            [("upool_d", tt_i)])

                def fm_glu(c, tt_i=tt_i, t0=t0, hT=hT, hk=hk):
                    st = stg[1]
                    psb_, psa_ = bank[2], bank[3]
                    for k in range(8):
                        mm(psb_, wB[:, 4 + c, k, :], hT[:, k, :], k == 0, k == 7, [hk, "wB"], [("bk", 2)])
                    act(tnh[c], psb_, AF.Tanh, [("bk", 2)], [("tnh", c)], scale=0.5)
                    for k in range(8):
                        mm(psa_, wB[:, 2 + c, k, :], hT[:, k, :], k == 0, k == 7, [hk, "wB"], [("bk", 3)])
                    stt(st[:, c, :], tnh[c], 1.0, psa_, ALU.add, ALU.mult, [("tnh", c), ("bk", 3)], [("stg", 1)])
                    if c == 1:
                        dma(aconv_d[:, :, t0:t0 + TT].rearrange("c p t -> p c t"), st, cst[1], [("stg", 1)],
                            [("aconv_d", tt_i)])
                fm = [lambda: fm_pool(0), lambda: fm_pool(1), lambda: fm_glu(0), lambda: fm_glu(1)]
                for sub in range(4):
                    si = tt_i * 4 + sub
                    psA, psF = bank[sub % 2], bank[4 + sub % 2]
                    kA, kF = ("bk", sub % 2), ("bk", 4 + sub % 2)
                    tk = tokst[sub % 2]
                    for k in range(8):
                        mm(psA[:, 0:448], hT[:, k, sub * 128:(sub + 1) * 128], wA[:, k, :], k == 0, k == 7,
                           [hk, "wA"], [kA])
                    for k in range(8):
                        mm(psF[:, 0:256], hT[:, k, sub * 128:(sub + 1) * 128], wF[:, k, :], k == 0, k == 7,
                           [hk, "wF"], [kF])
                    cp("act", ufour[:, si, :], psF[:, 0:256], [kF], [("ufour", si)])
                    sq = small[:, 40 + 2 * (sub % 2):42 + 2 * (sub % 2)]
                    junk = tnh[0][:, 0:256]
                    act(junk, psA[:, 0:256], AF.Square, [kA], [("tnh", 0), ("sq", sub % 2)],
                        accum=sq[:, 0:1])
                    act(junk[:, 0:128], psA[:, 256:384], AF.Square, [kA], [("tnh", 0), ("sq", sub % 2)],
                        accum=sq[:, 1:2])
                    rs = small[:, 48 + 2 * (sub % 2):50 + 2 * (sub % 2)]
                    ts("pool", rs[:, 0:1], sq[:, 0:1], 1.0 / 256, ALU.mult, [("sq", sub % 2)], [("rs", sub % 2)],
                       s2=1e-6, op1=ALU.add)
                    ts("pool", rs[:, 1:2], sq[:, 1:2], 1.0 / 128, ALU.mult, [("sq", sub % 2)], [("rs", sub % 2)],
                       s2=1e-6, op1=ALU.add)
                    rq = small[:, 56 + 2 * (sub % 2):58 + 2 * (sub % 2)]
                    tt("pool", rq, rs, mhalf[:, 0:2], ALU.pow, [("rs", sub % 2), "mhalf"], [("rq", sub % 2)])
                    act(tk[:, 0:256], psA[:, 0:256], AF.Copy, [kA, ("rq", sub % 2)], [("tk", sub % 2)],
                        scale=rq[:, 0:1])
                    act(tk[:, 256:384], psA[:, 256:384], AF.Copy, [kA, ("rq", sub % 2)], [("tk", sub % 2)],
                        scale=rq[:, 1:2])
                    rt = rtmp[sub % 2]
                    cosk = ropetok[:, si * 32:(si + 1) * 32]
                    sink = ropetok[:, (NS + si) * 32:(NS + si + 1) * 32]
                    tt("dve", rt[:, 0:32], psA[:, 416:448], sink, ALU.mult, [kA, "ropetok"], [("rt", sub % 2)])
                    tt("dve", rt[:, 32:64], psA[:, 384:416], cosk, ALU.mult, [kA, "ropetok"],
                       [("rt2", sub % 2)])
                    tt("dve", tk[:, 384:416], rt[:, 0:32], rt[:, 32:64], ALU.add, [("rt", sub % 2), ("rt2", sub % 2)],
                       [("tk", sub % 2)])
                    fm[sub]()
                    pt, ptk = pstb[sub % 2]
                    for j in range(3):
                        tr(pt[:, j * 128:(j + 1) * 128], tk[:, j * 128:(j + 1) * 128], [("tk", sub % 2)],
                           [ptk])
                    tr(pt[0:32, 384:512], tk[:, 384:416], [("tk", sub % 2)], [ptk])
                    tsl = slice(si * 128, (si + 1) * 128)
                    tt("dve", cqnT[:, :, tsl], pt[:, 0:256].rearrange("p (c t) -> p c t", c=2),
                       vec[:, vb + V_GQ:vb + V_GQ + 2].unsqueeze(2).broadcast_to([128, 2, 128]), ALU.mult,
                       [ptk, "vec"], [("cqnT", si)])
                    ts("dve", ckvnT[:, tsl], pt[:, 256:384], vec[:, vb + V_GKV:vb + V_GKV + 1], ALU.mult,
                       [ptk, "vec"], [("ckvnT", si)])
                    cp("dve", kropeT[0:32, tsl], pt[0:32, 384:512], [ptk], [("kropeT", si)])
            P.barrier()
            chan_phase()
            A.reset(p1_mark)
            tabs = [A.alloc(2 * 4 * TT * 2).rearrange("p (a i t) -> p a i t", a=2, i=4) for _ in range(3)]
            fev = [A.alloc(TT * 2) for _ in range(4)]
            gev = [A.alloc(TT * 2) for _ in range(2)]
            ofst = A.alloc(2 * TT * 2).rearrange("p (c t) -> p c t", c=2)
            wfo = A.alloc(2 * 256 * 2).rearrange("p (k n) -> p k n", k=2)
            ctab = [newchan() for _ in range(6)]
            cof = newchan()
            dma(wfo, wview(l, "FOURW", "(p k n) -> p k n", p=128, k=2), cof, ["wbf"], ["wfo"])
            U = A.alloc(2 * 528 * 2).rearrange("p (c t) -> p c t", c=2)
            p1 = A.alloc(2 * 528 * 4, F32).rearrange("p (c t) -> p c t", c=2)
            p2 = A.alloc(2 * 528 * 4, F32).rearrange("p (c t) -> p c t", c=2)
            p3 = A.alloc(528 * 4, F32)
            p4 = A.alloc(528 * 4, F32)
            pinvE = A.alloc(2 * 16 * 4, F32).rearrange("p (c t) -> p c t", c=2)
            pinvM = vec[:, vb + V_PINV:vb + V_PINV + 2]
            acc = A.alloc(2 * TT * 4, F32).rearrange("p (c t) -> p c t", c=2)
            pdf = A.alloc(2 * TT * 2).rearrange("p (c t) -> p c t", c=2)
            Ac = A.alloc(2 * 544 * 2).rearrange("p (c t) -> p c t", c=2)
            zz = A.alloc(2 * TT * 4, F32).rearrange("p (c t) -> p c t", c=2)
            dd = A.alloc(2 * TT * 4, F32).rearrange("p (c t) -> p c t", c=2)
            ybf = A.alloc(2 * TT * 2).rearrange("p (c t) -> p c t", c=2)
            ysq = A.alloc(2 * TT * 2).rearrange("p (c t) -> p c t", c=2)
            mean = A.alloc(TT * 4, F32)
            var = A.alloc(TT * 4, F32)
            rstd = A.alloc(TT * 4, F32)
            sbf = A.alloc(2 * TT * 2).rearrange("p (c t) -> p c t", c=2)
            opst = A.alloc(4 * TT * 2).rearrange("p (c t) -> p c t", c=4)
            wpool = A.alloc(2 * 128 * 2).rearrange("p (c n) -> p c n", c=2)
            wcpw = A.alloc(2 * 256 * 2).rearrange("p (k n) -> p k n", k=2)
            tnhb = [A.alloc(TT * 4, F32) for _ in range(2)]
            cbi = [newchan() for _ in range(2)]
            cbo = [newchan() for _ in range(2)]
            dma(wpool, wview(l, "POOLW", "(p c n) -> p c n", p=128, c=2), cof, ["wbf"], ["wpool"])
            dma(wcpw, wview(l, "CONVPW", "(p k n) -> p k n", p=128, k=2), cof, ["wbf"], ["wcpw"])
            dma(pinvE[:, :, 0:8], pinv_d[:, :, 0:8], cof, [], ["pinvE"])
            dma(pinvE[:, :, 8:16], pinv_d[:, :, S - 8:S], cof, [], ["pinvE"])
            dg = A.alloc(62 * 128 * 2).rearrange("p (i n) -> p i n", i=62)
            for i_ in range(62):
                ts("dve", dg[:, i_, :], ident, vder[:, vb + V_CW + i_:vb + V_CW + i_ + 1], ALU.mult,
                   ["ident", "vder"], ["dg"])
            def branch_pc(tb):
                t0 = tb * TT
                lo, hi = max(t0 - 8, 0), min(t0 + TT + 8, S)
                if t0 == 0:
                    memset("pool", U[:, :, 0:8], 0.0, ["U"])
                if t0 + TT == S:
                    memset("pool", U[:, :, 520:528], 0.0, ["U"])
                dma(U[:, :, lo - (t0 - 8):hi - (t0 - 8)], upool_d[:, :, lo:hi].rearrange("c p t -> p c t"), cbi[0],
                    [("upool_d", i) for i in range(NT)], ["U"])
                lo, hi = max(t0 - 15, 0), min(t0 + TT + 15, S)
                if t0 == 0:
                    memset("pool", Ac[:, :, 0:15], 0.0, ["Ac"])
                if t0 + TT == S:
                    memset("pool", Ac[:, :, 527:544], 0.0, ["Ac"])
                dma(Ac[:, :, lo - (t0 - 15):hi - (t0 - 15)], aconv_d[:, :, lo:hi].rearrange("c p t -> p c t"), cbi[1],
                    [("aconv_d", i) for i in range(NT)], ["Ac"])
                tt("pool", p1[:, :, 0:527], U[:, :, 0:527], U[:, :, 1:528], ALU.add, ["U"], ["p1"])
                tt("pool", p2[:, :, 0:525], p1[:, :, 0:525], p1[:, :, 2:527], ALU.add, ["p1"], ["p2"])
                tt("pool", p3[:, 0:521], p2[:, 1, 0:521], p2[:, 1, 4:525], ALU.add, ["p2"], ["p3"])
                tt("pool", p4[:, 0:513], p3[:, 0:513], p3[:, 8:521], ALU.add, ["p3"], ["p4"])
                srcs = ((0, 0, p1[0:64, 0, 7:519]), (0, 64, p2[64:128, 0, 6:518]), (1, 0, p3[0:64, 4:516]),
                        (1, 64, p4[64:128, 0:512]))
                for (c, pb_, sap) in srcs:
                    ts("pool", acc[pb_:pb_ + 64, c, :], sap, pinvM[pb_:pb_ + 64, c:c + 1], ALU.mult,
                       ["p1", "p2", "p3", "p4", "vec"], [("accp", c, pb_)], s2=1.0, op1=ALU.mult)
                    if t0 == 0:
                        tt("pool", acc[pb_:pb_ + 64, c, 0:8], sap[:, 0:8], pinvE[pb_:pb_ + 64, c, 0:8], ALU.mult,
                           ["p1", "p2", "p3", "p4", "pinvE"], [("accp", c, pb_)])
                    if t0 + TT == S:
                        tt("pool", acc[pb_:pb_ + 64, c, 504:512], sap[:, 504:512], pinvE[pb_:pb_ + 64, c, 8:16],
                           ALU.mult, ["p1", "p2", "p3", "p4", "pinvE"], [("accp", c, pb_)])
                tt("pool", pdf, acc, U[:, :, 8:520], ALU.subtract, [("accp", c, pb_) for (c, pb_, _) in srcs] + ["U"],
                   ["pdf"])
                yield
                for c in range(2):
                    mm(bank[6], wpool[:, c, :], pdf[:, c, :], True, True, ["pdf", "wpool"], [("bk", 6)])
                    ts("dve", opst[:, c, :], bank[6], vec[:, vb + V_PSC + c:vb + V_PSC + c + 1], ALU.mult,
                       [("bk", 6), "vec"], [("opst", 0)])
                dma(obr_d[0:2, :, t0:t0 + TT].rearrange("c p t -> p c t"), opst[:, 0:2, :], cbo[0], [("opst", 0)],
                    [("obr_pool", tb)])
                for c in range(2):
                    for k in range(CONV_K):
                        mm(bank[6], dg[:, c * 31 + k, :], Ac[:, c, k:k + TT], k == 0, k == CONV_K - 1, ["Ac", "dg"],
                           [("bk", 6)])
                    ts("dve", zz[:, c, :], bank[6], vec[:, vb + V_CB + c:vb + V_CB + c + 1], ALU.add,
                       [("bk", 6), "vec"], [("zz", c)])
                for c in range(2):
                    cp("act", ybf[:, c, :], zz[:, c, :], [("zz", c)], [("ybf", c)])
                    act(ysq[:, c, :], zz[:, c, :], AF.Square, [("zz", c)], [("ysq", c)])
                yield
                for c in range(2):
                    mm(bank[6], ones, ybf[:, c, :], c == 0, c == 1, [("ybf", c), "ones"], [("bk", 6)])
                act(mean, bank[6], AF.Copy, [("bk", 6)], ["mean"], scale=1.0 / 256)
                for c in range(2):
                    mm(bank[6], ones, ysq[:, c, :], c == 0, c == 1, [("ysq", c), "ones"], [("bk", 6)])
                tt("pool", rstd, mean, mean, ALU.mult, ["mean"], ["rstd"])
                stt(var, bank[6], 1.0 / 256, rstd, ALU.mult, ALU.subtract, [("bk", 6), "rstd"], ["var"])
                act(var, var, AF.Ln, ["var", "epsc"], ["var"], bias=epsc[:, 0:1])
                act(rstd, var, AF.Exp, ["var"], ["rstd"], scale=-0.5)
                for c in range(2):
                    tt("dve", dd[:, c, :], zz[:, c, :], mean, ALU.subtract, [("zz", c), "mean"], [("dd", c)])
                    tt("dve", dd[:, c, :], dd[:, c, :], rstd, ALU.mult, [("dd", c), "rstd"], [("dd", c)])
                    act(tnhb[c], dd[:, c, :], AF.Tanh, [("dd", c), "vder"], [("tnhb", c)],
                        scale=vder[:, vb + V_LNG + c:vb + V_LNG + c + 1],
                        bias=vder[:, vb + V_LNB + c:vb + V_LNB + c + 1])
                    ts("dve", zz[:, c, :], dd[:, c, :], vec[:, vb + V_LNG + c:vb + V_LNG + c + 1], ALU.mult,
                       [("dd", c), "vec"], [("zz", c)], s2=vec[:, vb + V_LNB + c:vb + V_LNB + c + 1], op1=ALU.add)
                    stt(sbf[:, c, :], tnhb[c], 1.0, zz[:, c, :], ALU.add, ALU.mult, [("tnhb", c), ("zz", c)],
                        [("sbf", c)])
                yield
                for co in range(2):
                    for k in range(2):
                        mm(bank[6], wcpw[:, k, co * 128:(co + 1) * 128], sbf[:, k, :], k == 0, k == 1,
                           [("sbf", k), "wcpw"], [("bk", 6)])
                    ts("dve", opst[:, 2 + co, :], bank[6], 0.5, ALU.mult, [("bk", 6), "vec"], [("opst", 1)],
                       s2=vec[:, vb + V_PWB + co:vb + V_PWB + co + 1], op1=ALU.add)
                dma(obr_d[6:8, :, t0:t0 + TT].rearrange("c p t -> p c t"), opst[:, 2:4, :], cbo[1], [("opst", 1)],
                    [("obr_conv", tb)])

            gi = 0
            bgens = []
            dpend = [None]
            for j in range(NT):
                bgens.append(branch_pc(j))
                ngrp = NS // 4
                for ig in range(ngrp):
                    for st_ in range(4):
                        if ig == min(ngrp - 1, 2 * st_ + 1):
                            t_ = j - 3 + st_
                            if 0 <= t_ < NT:
                                next(bgens[t_], None)
                    tb = tabs[gi % 3]
                    for a in range(2):
                        dma(tb[:, a].rearrange("p i t -> p (i t)"), dft_d[a, j, ig], ctab[(gi % 3) * 2 + a], [],
                            [("tab", gi % 3, a)])
                    for ii in range(4):
                        i = ig * 4 + ii
                        for c in range(2):
                            for a in range(2):
                                mm(bank[a * 2 + c], ufour[:, i, c * 128:(c + 1) * 128], tb[:, a, ii, :],
                                   i == 0, i == NS - 1, [("tab", gi % 3, a)], [("bk", a * 2 + c)])
                    gi += 1
                    if ig == 1 and dpend[0] is not None:
                        dpend[0]()
                        dpend[0] = None
                for a in range(2):
                    for c in range(2):
                        act(fev[a * 2 + c], bank[a * 2 + c], AF.Copy, [("bk", a * 2 + c)], [("fev", a * 2 + c)],
                            scale=float((S * 64) ** -0.5))

                def post_j(j=j):
                    for c in range(2):
                        mm(bank[4 + c], c64, fev[c], True, False, [("fev", c), "cbd"], [("bk", 4 + c)])
                        mm(bank[4 + c], ns64, fev[2 + c], False, True, [("fev", 2 + c), "cbd"], [("bk", 4 + c)])
                        cp("dve", gev[c], bank[4 + c], [("bk", 4 + c)], [("gev", c)])
                    for co in range(2):
                        for k in range(2):
                            mm(bank[7], wfo[:, k, co * 128:(co + 1) * 128], gev[k], k == 0, k == 1,
                               [("gev", k), "wfo"], [("bk", 7)])
                        cp("dve", ofst[:, co, :], bank[7], [("bk", 7)], [("ofst", 0)])
                    dma(obr_d[8:10, :, j * TT:(j + 1) * TT].rearrange("c p t -> p c t"), ofst, cof, [("ofst", 0)],
                        [("obr_four", j)])
                dpend[0] = post_j
            if dpend[0] is not None:
                dpend[0]()
                dpend[0] = None
            for i_ in range(NT, NT + 3):
                for st_ in range(4):
                    t_ = i_ - 3 + st_
                    if 0 <= t_ < NT:
                        next(bgens[t_], None)
            P.barrier()
            chan_phase()
            A.reset(pA_mark)
            wuq = A.alloc(2 * 768 * 2).rearrange("p (k n) -> p k n", k=2)
            wuqs = A.alloc(2 * 768 * 2).rearrange("p (k n) -> p k n", k=2)
            wukv = A.alloc(1024 * 2)
            KT = [A.alloc(S * 2) for _ in range(2)]
            VT = [A.alloc(NS * 128 * 2).rearrange("p (s c) -> p s c", c=128) for _ in range(2)]
            QT = [A.alloc(S * 2) for _ in range(2)]
            pT = [A.alloc(TT * 2) for _ in range(4)]
            rtab = [A.alloc(2 * TT * 4, F32).rearrange("p (a t) -> p a t", a=2) for _ in range(2)]
            rr1 = [A.alloc(TT * 4, F32) for _ in range(2)]
            rr2 = [A.alloc(TT * 4, F32) for _ in range(2)]
            rec = [A.alloc(TT * 4, F32) for _ in range(2)]
            oat = [A.alloc(TT * 2) for _ in range(2)]
            cwa = newchan()
            crt = [newchan() for _ in range(2)]
            coa = [newchan() for _ in range(2)]
            dma(wuq, wview(l, "WUQ", "(p k n) -> p k n", p=128, k=2), cwa, ["wbf"], ["wuq"])
            dma(wuqs, wview(l, "WUQS", "(p k n) -> p k n", p=128, k=2), cwa, ["wbf"], ["wuqs"])
            dma(wukv, wview(l, "WUKV", "(p n) -> p n", p=128), cwa, ["wbf"], ["wukv"])
            memset("pool", VT[0][:, :, 64:128], 1.0, [("VT", 0)])
            memset("pool", VT[1][:, :, 0:64], 1.0, [("VT", 1)])
            for hb in range(2):
                for j in range(NT):
                    cp("dve", KT[hb][64:96, j * TT:(j + 1) * TT], kropeT[0:32, j * TT:(j + 1) * TT], [],
                       [("KT", hb)])
            rti = [0]

            def build_head(h):
                hb = h % 2
                for j in range(NT):
                    mm(bank[5][0:64, :], wukv[:, h * 128:h * 128 + 64], ckvnT[:, j * TT:(j + 1) * TT], True, True,
                       ["wukv"], [("bk", 5)])
                    cp("dve", KT[hb][0:64, j * TT:(j + 1) * TT], bank[5][0:64, :], [("bk", 5)], [("KT", hb)])
                for g in range(NS // 8):
                    for ii in range(8):
                        i = g * 8 + ii
                        mm(bank[6][:, ii * 64:(ii + 1) * 64], ckvnT[:, i * 128:(i + 1) * 128],
                           wukv[:, h * 128 + 64:h * 128 + 128], True, True, ["wukv"], [("bk", 6)])
                    vo = 0 if hb == 0 else 64
                    cp("dve", VT[hb][:, g * 8:(g + 1) * 8, vo:vo + 64],
                       bank[6].rearrange("p (i c) -> p i c", i=8), [("bk", 6)], [("VT", hb)])
                for j in range(NT):
                    rb = rti[0] % 2
                    rti[0] += 1
                    dma(rtab[rb][64:96, :, :], ropeT_d[:, :, j * TT:(j + 1) * TT].rearrange("a d t -> d a t"),
                        crt[rb], [], [("rtab", rb)])
                    psA, psB = bank[5], bank[7]
                    for k in range(2):
                        mm(psA[0:96, :], wuq[:, k, h * 96:(h + 1) * 96], cqnT[:, k, j * TT:(j + 1) * TT], k == 0,
                           k == 1, ["wuq"], [("bk", 5)])
                    for k in range(2):
                        mm(psB[0:96, :], wuqs[:, k, h * 96:(h + 1) * 96], cqnT[:, k, j * TT:(j + 1) * TT], k == 0,
                           k == 1, ["wuqs"], [("bk", 7)])
                    qs = slice(j * TT, (j + 1) * TT)
                    cp("dve", QT[hb][0:64, qs], psA[0:64, :], [("bk", 5)], [("QT", hb)])
                    tt("dve", rr1[rb][64:96, :], psB[64:96, :], rtab[rb][64:96, 1, :], ALU.mult,
                       [("bk", 7), ("rtab", rb)], [("rr1", rb)])
                    tt("dve", rr2[rb][64:96, :], psA[64:96, :], rtab[rb][64:96, 0, :], ALU.mult,
                       [("bk", 5), ("rtab", rb)], [("rr2", rb)])
                    tt("dve", QT[hb][64:96, qs], rr1[rb][64:96, :], rr2[rb][64:96, :], ALU.add,
                       [("rr1", rb), ("rr2", rb)], [("QT", hb)])

            build_head(0)
            qi = 0
            pi = 0
            for h in range(8):
                hb = h % 2
                tot = NT * NS

                def score(s_, hb=hb):
                    j_, kc_ = divmod(s_, NS)
                    sb = s_ % 3
                    mm(bank[sb], KT[hb][0:96, kc_ * 128:(kc_ + 1) * 128], QT[hb][0:96, j_ * TT:(j_ + 1) * TT], True,
                       True, [("KT", hb), ("QT", hb)], [("bk", sb)])
                score(0)
                score(1)
                for s_ in range(tot):
                    j, kc = divmod(s_, NS)
                    ob = 3 + (qi + j) % 2
                    ok = ("bk", ob)
                    qs = slice(j * TT, (j + 1) * TT)
                    if s_ + 2 < tot:
                        score(s_ + 2)
                    pb = pi % 4
                    pi += 1
                    act(pT[pb], bank[s_ % 3], AF.Exp, [("bk", s_ % 3)], [("pT", pb)], scale=SCALE)
                    mm(bank[ob], VT[hb][:, kc, :], pT[pb], kc == 0, kc == NS - 1, [("VT", hb), ("pT", pb)], [ok])
                    if kc == NS - 1:
                        d0, s0 = (0, 64) if hb == 0 else (64, 0)
                        rb = (qi + j) % 2
                        P.op("dve", (lambda o_, i_: (lambda e: e.reciprocal(out=o_, in_=i_)))(
                            rec[rb][d0:d0 + 64, :], bank[ob][s0:s0 + 64, :]), r=[ok], w=[("rec", rb)])
                        tt("dve", oat[rb][d0:d0 + 64, :], bank[ob][d0:d0 + 64, :], rec[rb][d0:d0 + 64, :], ALU.mult,
                           [ok, ("rec", rb)], [("oat", rb)])
                        dma(obr_d[2 + h // 2, d0:d0 + 64, qs], oat[rb][d0:d0 + 64, :], coa[rb], [("oat", rb)],
                            [("obr_attn", h // 2, j, hb)])
                        if j == NT // 2 - 1 and h + 1 < 8:
                            build_head(h + 1)
                qi += NT
            P.barrier()
            chan_phase()
            A.reset(base_mark)
            wout = A.alloc(8 * 1024 * 2).rearrange("p (k n) -> p k n", k=8)
            xbs = [[A.alloc(D * 4, F32) for _ in range(4)] for _ in range(2)]
            xns = [A.alloc(D * 2) for _ in range(4)]
            hTs = [A.alloc(8 * TT * 2).rearrange("p (k t) -> p k t", k=8) for _ in range(2)]
            Abufs = [A.alloc(10 * TT * 2).rearrange("p (c t) -> p c t", c=10) for _ in range(2)]
            oin = A.alloc(10 * TT * 2).rearrange("p (c t) -> p c t", c=10)
            mT = A.alloc(8 * TT * 2).rearrange("p (k t) -> p k t", k=8)
            wring = [A.alloc(4 * 1024 * 2).rearrange("p (c k n) -> p c k n", c=4, k=8) for _ in range(3)]
            wupr = [A.alloc(10 * 128 * 2).rearrange("p (k n) -> p k n", k=10) for _ in range(2)]
            tnh = [A.alloc(TT * 4, F32) for _ in range(4)]
            tmp = [A.alloc(TT * 4, F32) for _ in range(4)]
            s01 = A.alloc(TT * 4, F32)
            s23 = A.alloc(TT * 4, F32)
            otmp = [A.alloc(D * 4, F32) for _ in range(2)]
            pstb = [(bank[6].bitcast(BF16), ("bk", 6)), (bank[7].bitcast(BF16), ("bk", 7))]
            cw2 = newchan()
            cxs = [[newchan() for _ in range(4)] for _ in range(2)]
            cwr = [newchan() for _ in range(3)]
            cwu = [newchan() for _ in range(2)]
            cin = newchan()
            cout = [newchan() for _ in range(2)]
            dma(wout, wview(l, "WOUT", "(p k n) -> p k n", p=128, k=8), cw2, ["wbf"], ["wout"])
            WGv = wview(l, "WG", "(c p k n) -> p c k n", c=10, p=128, k=8)
            WMv = wview(l, "WM", "(j i p k n) -> j p i k n", j=8, i=4, p=128, k=8)
            WUPv = wview(l, "WUP", "(j p k n) -> j p k n", j=8, p=128, k=10)
            st2 = dict(wri=0, gbank=0, oi=0)
            nkeys = [("xn", i) for i in range(4)]

            def prep_a(tt_i):
                pb_ = tt_i % 2
                make_hT_a(l, src, R0 + tt_i * TT, xbs[pb_], xns, cxs[pb_], [("xb", pb_, i) for i in range(4)], nkeys,
                          16 * pb_)

            def prep_b(tt_i):
                pb_ = tt_i % 2
                t0 = tt_i * TT
                hT = hTs[pb_]
                hk = ("hT", id(hT))
                Abuf = Abufs[pb_]
                make_hT_b(l, xns, nkeys, hT, pstb)
                dma(oin, obr_d[:, :, t0:t0 + TT].rearrange("c p t -> p c t"), cin, [], ["oin"])
                for g3, (c0, c1) in enumerate(((0, 4), (4, 8), (8, 10))):
                    wri = st2["wri"]
                    wr = wring[wri % 3]
                    wk = ("wring", wri % 3)
                    dma(wr[:, 0:c1 - c0], WGv[:, c0:c1], cwr[wri % 3], ["wbf"], [wk])
                    for c in range(c0, c1):
                        gb = st2["gbank"]
                        ps = bank[gb % 2]
                        pk = ("bk", gb % 2)
                        tb = gb % 4
                        st2["gbank"] += 1
                        for k in range(8):
                            mm(ps, wr[:, c - c0, k, :], hT[:, k, :], k == 0, k == 7, [hk, wk], [pk])
                        act(tnh[tb], ps, AF.Tanh, [pk], [("tnh", tb)], scale=0.5)
                        stt(Abuf[:, c, :], tnh[tb], 1.0, ps, ALU.add, ALU.mult, [("tnh", tb), pk], [("A", pb_, c)])
                        stt(Abuf[:, c, :], oin[:, c, :], 0.25, Abuf[:, c, :], ALU.mult, ALU.mult,
                            ["oin", ("A", pb_, c)], [("A", pb_, c)])
                    st2["wri"] += 1

            def merge(tt_i, mid=None):
                pb_ = tt_i % 2
                t0 = tt_i * TT
                hT = hTs[pb_]
                hk = ("hT", id(hT))
                Abuf = Abufs[pb_]
                xb = xbs[pb_]
                kk0 = (0, 2, 6, 8)
                nk = (2, 4, 2, 2)
                for j in range(8):
                    wri = st2["wri"]
                    wr = wring[wri % 3]
                    wk = ("wring", wri % 3)
                    dma(wr, WMv[j], cwr[wri % 3], ["wbf"], [wk])
                    wu = wupr[j % 2]
                    uk = ("wupr", j % 2)
                    dma(wu, WUPv[j], cwu[j % 2], ["wbf"], [uk])
                    for i in range(4):
                        gb = st2["gbank"]
                        ps = bank[gb % 2]
                        pk = ("bk", gb % 2)
                        tb = gb % 4
                        st2["gbank"] += 1
                        for k in range(8):
                            mm(ps, wr[:, i, k, :], hT[:, k, :], k == 0, k == 7, [hk, wk], [pk])
                        act(tnh[tb], ps, AF.Tanh, [pk, "vder"], [("tnh", tb)], scale=0.5,
                            bias=vder[:, vb + V_GB + i * 8 + j:vb + V_GB + i * 8 + j + 1])
                        pu = bank[2 + i % 2]
                        pku = ("bk", 2 + i % 2)
                        for k in range(nk[i]):
                            mm(pu, wu[:, kk0[i] + k, :], Abuf[:, kk0[i] + k, :], k == 0, k == nk[i] - 1,
                               [uk, ("A", pb_, kk0[i] + k)], [pku])
                        stt(tmp[i], tnh[tb], 1.0, pu, ALU.add, ALU.mult, [("tnh", tb), pku], [("tmp", i)])
                    tt("pool", s01, tmp[0], tmp[1], ALU.add, [("tmp", 0), ("tmp", 1)], ["s01"])
                    tt("pool", s23, tmp[2], tmp[3], ALU.add, [("tmp", 2), ("tmp", 3)], ["s23"])
                    tt("pool", mT[:, j, :], s01, s23, ALU.add, ["s01", "s23"], [("mT", j)])
                    st2["wri"] += 1
                    if j == 3 and mid is not None:
                        mid()
                for sub in range(4):
                    pp = 4 if sub % 2 == 0 else 2
                    po = psum_t[:, pp * 512:(pp + 2) * 512]
                    kx = ("xb", pb_, sub)
                    for n in range(2):
                        for k in range(8):
                            mm(bank[pp + n], mT[:, k, sub * 128:(sub + 1) * 128], wout[:, k, n * 512:(n + 1) * 512],
                               k == 0, k == 7, [("mT", k), "wout"], [("bk", pp + n)])
                    so = 32 + 8 * (sub % 2)
                    sq = small[:, so:so + 2]
                    for n in range(2):
                        act(tnh[n][:, 0:512], bank[pp + n], AF.Square, [("bk", pp + n)], [("tnh", n), ("osq", sub % 2, n)],
                            accum=sq[:, n:n + 1])
                    tt("pool", small[:, so + 2:so + 3], sq[:, 0:1], sq[:, 1:2], ALU.add,
                       [("osq", sub % 2, 0), ("osq", sub % 2, 1)], [("os1", sub % 2)])
                    ts("pool", small[:, so + 3:so + 4], small[:, so + 2:so + 3], 1.0 / D, ALU.mult, [("os1", sub % 2)],
                       [("os2", sub % 2)], s2=1e-6, op1=ALU.add)
                    tt("pool", small[:, so + 4:so + 5], small[:, so + 3:so + 4], mhalf[:, 0:1], ALU.pow,
                       [("os2", sub % 2), "mhalf"], [("os3", sub % 2)])
                    ot = otmp[sub % 2]
                    stt(ot, po, small[:, so + 4:so + 5], gpost, ALU.mult, ALU.mult,
                        [("bk", pp), ("bk", pp + 1), ("os3", sub % 2), "gpost"], [("otmp", sub % 2)])
                    tt("dve", xb[sub], ot, xb[sub], ALU.add, [("otmp", sub % 2), kx], [kx])
                    dma(dst[R0 + t0 + sub * 128:R0 + t0 + (sub + 1) * 128, :], xb[sub], cout[sub % 2], [kx],
                        [("dst", slot, l)])

            prep_a(0)
            prep_b(0)
            for tt_i in range(NT):
                if tt_i + 1 < NT:
                    prep_a(tt_i + 1)
                    merge(tt_i, (lambda t=tt_i + 1: prep_b(t)))
                else:
                    merge(tt_i)
            P.barrier()

    P.finalize()
    with nc.Block() as block:
        @block.tensor
        def _(e):
            P.replay("pe", e, sems)

        @block.scalar
        def _(e):
            P.replay("act", e, sems)

        @block.vector
        def _(e):
            P.replay("dve", e, sems)

        @block.gpsimd
        def _(e):
            P.replay("pool", e, sems)

        @block.sync
        def _(e):
            P.replay("sp", e, sems)
    es.close()
    return nc


def _pack_layer(l, w):
    f = np.float32
    out = np.zeros(W_LAYER, f)

    def put(name, arr):
        a = np.ascontiguousarray(arr, dtype=f).reshape(-1)
        assert a.size == dict(W_SIZES)[name], (name, a.size)
        out[W_OFF[name]:W_OFF[name] + a.size] = a
    w_in = w["w_in"][l]
    wk = w_in.reshape(8, 128, NIN)
    colsA = np.concatenate([np.arange(OFF_CQ, OFF_CQ + 256), np.arange(OFF_KVA, OFF_KVA + 160),
                            np.arange(OFF_KVA + 144, OFF_KVA + 160), np.arange(OFF_KVA + 128, OFF_KVA + 144)])
    put("WA", wk[:, :, colsA].transpose(1, 0, 2))
    put("WF", wk[:, :, OFF_UFOUR:OFF_UFOUR + 256].transpose(1, 0, 2))
    colsB = np.concatenate([np.arange(OFF_UPOOL, OFF_UPOOL + 256), np.arange(OFF_UCONV, OFF_UCONV + 512)])
    put("WB", wk[:, :, colsB].reshape(8, 128, 6, 128).transpose(2, 1, 0, 3))
    put("WG", wk[:, :, OFF_GPOOL:OFF_GPOOL + 1280].reshape(8, 128, 10, 128).transpose(2, 1, 0, 3))
    put("WM", wk[:, :, OFF_MERGE:].reshape(8, 128, 4, 8, 128).transpose(3, 2, 1, 0, 4))
    up = np.concatenate([w["w_up_pool"][l], w["w_up_attn"][l], w["w_up_conv"][l], w["w_up_fourier"][l]], 0)
    put("WUP", up.reshape(10, 128, 8, 128).transpose(2, 1, 0, 3))
    put("WOUT", w["w_out"][l].reshape(8, 128, 1024).transpose(1, 0, 2))
    uq = w["w_uq"][l]
    put("WUQ", uq.reshape(2, 128, 768).transpose(1, 0, 2))
    uqs = uq.reshape(256, 8, 96).copy()
    uqs[:, :, 64:80], uqs[:, :, 80:96] = uq.reshape(256, 8, 96)[:, :, 80:96], uq.reshape(256, 8, 96)[:, :, 64:80]
    put("WUQS", uqs.reshape(2, 128, 768).transpose(1, 0, 2))
    put("WUKV", w["w_ukv"][l])
    pw = np.zeros((128, 2, 128), f)
    for c in range(2):
        pw[0:64, c, 0:64] = w["pool_w"][l][2 * c]
        pw[64:128, c, 64:128] = w["pool_w"][l][2 * c + 1]
    put("POOLW", pw)
    put("CONVPW", w["conv_pw_w"][l].reshape(2, 128, 256).transpose(1, 0, 2))
    put("FOURW", w["fourier_w"][l].reshape(2, 128, 256).transpose(1, 0, 2))
    return out


def _pack_vecs(l, w):
    v = np.zeros((128, NV), np.float32)
    v[:, V_GPRE:V_GPRE + 8] = w["pre_norm_g"][l].reshape(8, 128).T
    v[:, V_GQ:V_GQ + 2] = w["q_norm_g"][l].reshape(2, 128).T
    v[:, V_GKV] = w["kv_norm_g"][l]
    v[:, V_GB:V_GB + 32] = w["gate_b"][l].reshape(32, 128).T
    v[:, V_PSC:V_PSC + 2] = w["pool_scale"][l].reshape(2, 128).T
    v[:, V_CW:V_CW + 62] = w["conv_w"][l].reshape(31, 2, 128).transpose(2, 1, 0).reshape(128, 62)
    v[:, V_CB:V_CB + 2] = w["conv_b"][l].reshape(2, 128).T
    v[:, V_LNG:V_LNG + 2] = w["conv_ln_g"][l].reshape(2, 128).T
    v[:, V_LNB:V_LNB + 2] = w["conv_ln_b"][l].reshape(2, 128).T
    v[:, V_PWB:V_PWB + 2] = w["conv_pw_b"][l].reshape(2, 128).T
    for c in range(2):
        v[0:64, V_PINV + c] = 1.0 / (2, 4, 8, 16)[2 * c]
        v[64:128, V_PINV + c] = 1.0 / (2, 4, 8, 16)[2 * c + 1]
    return v


_CONST_CACHE = {}


def _consts(S):
    if S in _CONST_CACHE:
        return _CONST_CACHE[S]
    NS = S // 128
    f = np.float32
    inv_freq = (1.0 / (f(10000.0) ** (np.arange(0, 32, 2, dtype=f) / f(32)))).astype(f)
    ang = (np.arange(S, dtype=f)[:, None] * inv_freq[None, :]).astype(f)
    ang = np.concatenate([ang, ang], -1)
    cos = np.cos(ang).astype(f)
    sin = np.sin(ang).astype(f)
    sinS = sin.copy()
    sinS[:, 0:16] *= -1
    ropetok = np.concatenate([cos.reshape(NS, 128, 32).transpose(1, 0, 2).reshape(128, NS * 32),
                              sinS.reshape(NS, 128, 32).transpose(1, 0, 2).reshape(128, NS * 32)], 1)
    ropeT = np.stack([cos.T, sinS.T], 0).astype(f)
    idx = (np.arange(S, dtype=np.int64)[:, None] * np.arange(S, dtype=np.int64)[None, :]) % S
    angd = idx.astype(np.float64) * (2 * np.pi / S)
    dft = np.stack([np.cos(angd).astype(NPBF), np.sin(angd).astype(NPBF)], 0)
    dft = np.ascontiguousarray(dft.reshape(2, S // 512, 4, 128, S // TT, TT).transpose(0, 4, 1, 3, 2, 5)).reshape(
        2, S // TT, S // 512, 128, 4 * TT)
    a64 = (np.arange(64)[:, None] * np.arange(64)[None, :]) % 64 * (2 * np.pi / 64)
    cbd = np.zeros((128, 512), np.float64)
    for b in range(2):
        cbd[b * 64:(b + 1) * 64, b * 64:(b + 1) * 64] = np.cos(a64)
        cbd[b * 64:(b + 1) * 64, 128 + b * 64:128 + (b + 1) * 64] = -np.sin(a64)
    cbd[:, 256:384] = 1.0
    cbd[:, 384:512] = np.eye(128)
    t = np.arange(S)
    pinv = np.zeros((128, 2, S), f)
    for g, wdw in enumerate((2, 4, 8, 16)):
        lo = np.clip(t - wdw // 2, 0, S)
        hi = np.clip(t + wdw // 2, 0, S)
        pinv[(g % 2) * 64:(g % 2) * 64 + 64, g // 2, :] = (1.0 / (hi - lo).astype(f))[None, :]
    res = dict(ropetok=np.ascontiguousarray(ropetok, dtype=f), ropeT=ropeT, dft=dft, cbd=cbd.astype(NPBF), pinv=pinv)
    _CONST_CACHE[S] = res
    return res


_NC_CACHE = {}


def run(xseqs, w, S, ncores, nslot):
    key = (S, nslot)
    if key not in _NC_CACHE:
        _NC_CACHE[key] = build(S, nslot)
    nc = _NC_CACHE[key]
    wp = np.concatenate([_pack_layer(l, w) for l in range(DEPTH)])
    vv = np.concatenate([_pack_vecs(l, w) for l in range(DEPTH)], 1)
    cst = _consts(S)
    gpost = np.ascontiguousarray(w["post_norm_g"], dtype=np.float32)
    in_maps = []
    for c in range(ncores):
        m = dict(xin=np.ascontiguousarray(xseqs[c * nslot:(c + 1) * nslot].reshape(nslot * S, D)), wpack=wp, vecs=vv,
                 gpost=gpost, ropetok=cst["ropetok"], ropeT=cst["ropeT"], dft=cst["dft"], cbd=cst["cbd"],
                 pinv=cst["pinv"])
        in_maps.append(m)
    res = run_bass_kernel_spmd(nc, in_maps, core_ids=list(range(ncores)))
    return np.stack([np.asarray(r["yout"]).reshape(nslot, S, D) for r in res.results], 0).reshape(
        ncores * nslot, S, D)


def kernel(**inputs):
    w = {k: np.asarray(v, dtype=np.float32) for k, v in inputs.items() if k not in ("x_prompt", "x_sample")}
    xp = np.asarray(inputs["x_prompt"], dtype=np.float32)
    xs = np.asarray(inputs["x_sample"], dtype=np.float32)
    S = xp.shape[1]
    seqs = np.concatenate([xp, xs], 0)
    nseq = seqs.shape[0]
    nslot = 3
    order = list(range(nseq)) + [0] * (NCORES * nslot - nseq)
    xin = seqs[order]
    y = run(xin, w, S, NCORES, nslot)
    y = y[:nseq]
    return (np.ascontiguousarray(y[:xp.shape[0]]), np.ascontiguousarray(y[xp.shape[0]:]))
```

```python
import numpy as np
import ml_dtypes
from contextlib import ExitStack
import concourse.bass as bass
import concourse.mybir as mybir
from concourse.bass_utils import run_bass_kernel_spmd

F32 = mybir.dt.float32
BF16 = mybir.dt.bfloat16
AF = mybir.ActivationFunctionType
ALU = mybir.AluOpType
NPBF = ml_dtypes.bfloat16

D = 1024
NIN = 6816
DEPTH = 2
NCORES = 8
TT = 512
CONV_K = 31
OFF_UPOOL, OFF_CQ, OFF_KVA, OFF_UCONV, OFF_UFOUR = 0, 256, 512, 672, 1184
OFF_GPOOL, OFF_GATTN, OFF_GCONV, OFF_GFOUR, OFF_MERGE = 1440, 1696, 2208, 2464, 2720
SCALE = 96.0 ** -0.5

W_SIZES = [("WA", 128 * 8 * 448), ("WF", 128 * 8 * 256), ("WB", 6 * 128 * 8 * 128), ("WG", 10 * 128 * 8 * 128),
           ("WM", 32 * 128 * 8 * 128), ("WUP", 8 * 128 * 10 * 128), ("WOUT", 128 * 8 * 1024),
           ("WUQ", 128 * 2 * 768), ("WUQS", 128 * 2 * 768), ("WUKV", 128 * 1024), ("POOLW", 128 * 2 * 128),
           ("CONVPW", 128 * 2 * 256), ("FOURW", 128 * 2 * 256)]
W_OFF = {}
_o = 0
for _n, _s in W_SIZES:
    W_OFF[_n] = _o
    _o += _s
W_LAYER = ((_o + 2047) // 2048) * 2048
V_GPRE, V_GQ, V_GKV, V_GB, V_PSC, V_CW, V_CB, V_LNG, V_LNB, V_PWB = 0, 8, 10, 11, 43, 45, 107, 109, 111, 113
V_PINV = 116
NV = 118


class Chan:
    def __init__(self, sem):
        self.sem = sem
        self.cnt = 0
        self.last = None


class Op:
    __slots__ = ("eng", "fn", "deps", "is_dma", "chan", "val", "sig", "need")


class Prog:
    ENG = ("pe", "act", "dve", "pool", "sp")

    def __init__(self):
        self.q = {e: [] for e in self.ENG}
        self.lastw = {}
        self.rd = {}
        self.chans = []
        self.lastreal = {}

    def chan(self, sem):
        c = Chan(sem)
        self.chans.append(c)
        return c

    def op(self, eng, fn, r=(), w=(), chan=None):
        o = Op()
        o.eng, o.fn, o.is_dma, o.chan, o.sig, o.need, o.val = eng, fn, chan is not None, chan, 0, False, 0
        deps = {}

        def add(d, raw):
            if d is None:
                return
            if d.is_dma or o.is_dma or d.eng != eng or raw or eng != "pe":
                deps[id(d)] = d
        for k in r:
            add(self.lastw.get(k), True)
        for k in w:
            add(self.lastw.get(k), False)
            rr = self.rd.get(k)
            if rr:
                for d in rr.values():
                    add(d, False)
        if chan is not None:
            if chan.last is not None:
                deps[id(chan.last)] = chan.last
            chan.cnt += 16
            o.val = chan.cnt
            chan.last = o
        o.deps = list(deps.values())
        for k in r:
            rr = self.rd.get(k)
            if rr is None:
                rr = self.rd[k] = {}
            rr[("d", id(o)) if o.is_dma else eng] = o
        for k in w:
            self.lastw[k] = o
            self.rd[k] = {}
        self.q[eng].append(o)
        if not o.is_dma:
            self.lastreal[eng] = o
        return o

    def barrier(self):
        deps = [d for d in self.lastreal.values()] + [c.last for c in self.chans if c.last is not None]
        for e in self.ENG:
            o = Op()
            o.eng, o.fn, o.is_dma, o.chan, o.sig, o.need, o.val = e, None, False, None, 0, False, 0
            o.deps = [d for d in deps if d.is_dma or d.eng != e]
            self.q[e].append(o)
        self.lastw.clear()
        self.rd.clear()

    def finalize(self):
        for e in self.ENG:
            for o in self.q[e]:
                for d in o.deps:
                    d.need = True
        for e in self.ENG:
            c = 0
            for o in self.q[e]:
                if o.is_dma or o.fn is None:
                    continue
                if o.need:
                    c += 1
                    o.sig = c

    def replay(self, ename, e, sems):
        waited = {}
        for o in self.q[ename]:
            need = {}
            for d in o.deps:
                if d.is_dma:
                    sem, v = d.chan.sem, d.val
                else:
                    sem, v = sems[d.eng], d.sig
                k = id(sem)
                if waited.get(k, 0) >= v:
                    continue
                if k not in need or need[k][1] < v:
                    need[k] = (sem, v)
            for k, (sem, v) in need.items():
                e.wait_ge(sem, v)
                waited[k] = v
            if o.fn is None:
                continue
            ins = o.fn(e)
            if o.is_dma:
                ins.then_inc(o.chan.sem, 16)
            elif o.need:
                ins.then_inc(sems[o.eng], 1)


class Arena:
    def __init__(self, t, nbytes):
        self.t = t
        self.n = nbytes
        self.o = 0

    def alloc(self, nbytes, dtype=BF16):
        req = nbytes
        nbytes = (nbytes + 31) // 32 * 32
        assert self.o + nbytes <= self.n, f"arena overflow {self.o}+{nbytes}>{self.n}"
        ap = self.t[:, self.o // 2:(self.o + req) // 2]
        self.o += nbytes
        if dtype == F32:
            ap = ap.bitcast(F32)
        return ap

    def mark(self):
        return self.o

    def reset(self, m):
        self.o = m


def build(S, NSLOT, depth=DEPTH, dbg=False):
    NT = S // TT
    NS = S // 128
    nc = bass.Bass("TRN2", target_bir_lowering=False)
    P = Prog()
    es = ExitStack()
    xin = nc.dram_tensor("xin", [NSLOT * S, D], F32, kind="ExternalInput").ap()
    wpack = nc.dram_tensor("wpack", [DEPTH * W_LAYER], F32, kind="ExternalInput").ap()
    vecs = nc.dram_tensor("vecs", [128, DEPTH * NV], F32, kind="ExternalInput").ap()
    gpost_d = nc.dram_tensor("gpost", [DEPTH, D], F32, kind="ExternalInput").ap()
    ropetok_d = nc.dram_tensor("ropetok", [128, 2 * NS * 32], F32, kind="ExternalInput").ap()
    ropeT_d = nc.dram_tensor("ropeT", [2, 32, S], F32, kind="ExternalInput").ap()
    dft_d = nc.dram_tensor("dft", [2, S // TT, S // 512, 128, 4 * TT], BF16, kind="ExternalInput").ap()
    cbd_d = nc.dram_tensor("cbd", [128, 4 * 128], BF16, kind="ExternalInput").ap()
    pinv_d = nc.dram_tensor("pinv", [128, 2, S], F32, kind="ExternalInput").ap()
    yout = nc.dram_tensor("yout", [NSLOT * S, D], F32, kind="ExternalOutput").ap()
    wbf = nc.dram_tensor("wbf", [DEPTH * W_LAYER], BF16, kind="Internal").ap()
    x1 = nc.dram_tensor("x1", [NSLOT * S, D], F32, kind="Internal").ap()
    sk = "ExternalOutput" if dbg else "Internal"
    upool_d = nc.dram_tensor("upool_s", [2, 128, S], BF16, kind=sk).ap()
    aconv_d = nc.dram_tensor("aconv_s", [2, 128, S], BF16, kind=sk).ap()
    obr_d = nc.dram_tensor("obr_s", [10, 128, S], BF16, kind=sk).ap()

    ARENA_BYTES = 188 * 1024
    arena_t = es.enter_context(nc.sbuf_tensor("arena", [128, ARENA_BYTES // 2], BF16))
    psum_t = es.enter_context(nc.psum_tensor("psum", [128, 4096], F32))
    A = Arena(arena_t, ARENA_BYTES)
    bank = [psum_t[:, b * 512:(b + 1) * 512] for b in range(8)]
    sems = {e: es.enter_context(nc.semaphore("s_" + e)) for e in ("pe", "act", "dve", "pool")}
    nchan = [0]

    def newchan_raw():
        nchan[0] += 1
        return P.chan(es.enter_context(nc.semaphore("ch%d" % nchan[0])))
    chpool = [newchan_raw() for _ in range(20)]
    chidx = [0]

    def newchan():
        c = chpool[chidx[0] % len(chpool)]
        chidx[0] += 1
        return c

    def chan_phase():
        chidx[0] = 0

    def mm(out, lhsT, rhs, st, sp, r, w):
        P.op("pe", lambda e: e.matmul(out, lhsT=lhsT, rhs=rhs, start=st, stop=sp), r=r, w=w)

    def tr(out, in_, r, w):
        P.op("pe", lambda e: e.transpose(out=out, in_=in_, identity=ident), r=list(r) + ["ident"], w=w)

    def act(out, in_, func, r, w, scale=1.0, bias=None, accum=None):
        kw = {}
        if bias is not None:
            kw["bias"] = bias
        if accum is not None:
            kw["accum_out"] = accum
        P.op("act", lambda e: e.activation(out=out, in_=in_, func=func, scale=scale, **kw), r=r, w=w)

    def tt(eng, out, a, b, op, r, w):
        P.op(eng, lambda e: e.tensor_tensor(out=out, in0=a, in1=b, op=op), r=r, w=w)

    def ts(eng, out, a, s1, op0, r, w, s2=None, op1=None):
        if op1 is None:
            P.op(eng, lambda e: e.tensor_scalar(out=out, in0=a, scalar1=s1, scalar2=None, op0=op0), r=r, w=w)
        else:
            P.op(eng, lambda e: e.tensor_scalar(out=out, in0=a, scalar1=s1, scalar2=s2, op0=op0, op1=op1), r=r, w=w)

    def stt(out, in0, scalar, in1, op0, op1, r, w):
        P.op("dve", lambda e: e.scalar_tensor_tensor(out=out, in0=in0, scalar=scalar, in1=in1, op0=op0, op1=op1),
             r=r, w=w)

    def cp(eng, out, in_, r, w):
        if eng == "act":
            act(out, in_, AF.Copy, r, w)
        else:
            P.op(eng, lambda e: e.tensor_copy(out=out, in_=in_), r=r, w=w)

    def dma(out, in_, chan, r, w, eng="sp"):
        P.op(eng, lambda e: e.dma_start(out=out, in_=in_), r=r, w=w, chan=chan)

    def memset(eng, ap, val, w):
        P.op(eng, lambda e: e.memset(ap, val), r=(), w=w)

    ident = A.alloc(256)
    ones = A.alloc(256)
    cbd = A.alloc(4 * 256)
    c64 = cbd[:, 0:128]
    ns64 = cbd[:, 128:256]
    vec = A.alloc(DEPTH * NV * 4, F32)
    vder = A.alloc(DEPTH * NV * 4, F32)
    ropetok = A.alloc(2 * NS * 32 * 4, F32)
    mhalf = A.alloc(TT * 4, F32)
    gpost = A.alloc(D * 4, F32)
    small = A.alloc(96 * 4, F32)
    epsc = A.alloc(32, F32)
    c_const = newchan_raw()
    dma(ident, cbd_d[:, 384:512], c_const, [], ["ident"])
    dma(ones, cbd_d[:, 256:384], c_const, [], ["ones"])
    dma(cbd[:, 0:256], cbd_d[:, 0:256], c_const, [], ["cbd"])
    dma(vec, vecs[:, :], c_const, [], ["vec"])
    dma(ropetok, ropetok_d[:, :], c_const, [], ["ropetok"])
    memset("pool", mhalf, -0.5, ["mhalf"])
    memset("pool", epsc, 1e-5, ["epsc"])
    for l in range(DEPTH):
        b = l * NV
        for (o0, o1, f) in ((V_GB, V_GB + 32, 0.5), (V_CW, V_CW + 62, 0.5),
                            (V_LNG, V_LNG + 2, 0.5), (V_LNB, V_LNB + 2, 0.5)):
            ts("pool", vder[:, b + o0:b + o1], vec[:, b + o0:b + o1], f, ALU.mult, ["vec"], ["vder"])

    c_cast = newchan_raw()
    ROWS = DEPTH * W_LAYER // 2048
    wsrc = wpack.rearrange("(r c) -> r c", c=2048)
    wdst = wbf.rearrange("(r c) -> r c", c=2048)
    r0 = 0
    while r0 < ROWS:
        r1 = min(ROWS, r0 + 2048)
        dma(wdst[r0:r1, :], wsrc[r0:r1, :], c_cast, [], ["wbf"], eng="pool")
        r0 = r1
    P.barrier()
    base_mark = A.mark()

    def wview(l, name, pattern, **kw):
        o = l * W_LAYER + W_OFF[name]
        n = dict(W_SIZES)[name]
        return wbf[o:o + n].rearrange(pattern, **kw)

    def make_hT_a(l, src, row0, xb, xn, cx, xkeys, nkeys, so):
        for sub in range(4):
            kx = xkeys[sub]
            dma(xb[sub], src[row0 + sub * 128: row0 + (sub + 1) * 128, :], cx[sub], [], [kx])
            ssq = small[:, so + sub:so + sub + 1]
            act(xn[sub], xb[sub], AF.Square, [kx], [nkeys[sub], ("ssq", so, sub)], accum=ssq)
            ts("pool", small[:, so + 4 + sub:so + 5 + sub], ssq, 1.0 / D, ALU.mult, [("ssq", so, sub)],
               [("ssq2", so, sub)], s2=1e-6, op1=ALU.add)
            tt("pool", small[:, so + 8 + sub:so + 9 + sub], small[:, so + 4 + sub:so + 5 + sub], mhalf[:, 0:1],
               ALU.pow, [("ssq2", so, sub), "mhalf"], [("rstd", so, sub)])
            act(xn[sub], xb[sub], AF.Copy, [kx, ("rstd", so, sub)], [nkeys[sub]],
                scale=small[:, so + 8 + sub:so + 9 + sub])

    def make_hT_b(l, xn, nkeys, hT, pstb):
        vb = l * NV
        for sub in range(4):
            pt, ptk = pstb[sub % 2]
            for k in range(8):
                tr(pt[:, k * 128:(k + 1) * 128], xn[sub][:, k * 128:(k + 1) * 128], [nkeys[sub]], [ptk])
            tt("dve", hT[:, :, sub * 128:(sub + 1) * 128], pt.rearrange("p (k t) -> p k t", k=8),
               vec[:, vb + V_GPRE:vb + V_GPRE + 8].unsqueeze(2).broadcast_to([128, 8, 128]), ALU.mult,
               [ptk, "vec"], [("hT", id(hT))])

    for slot in range(NSLOT):
        for l in range(depth):
            src = xin if l == 0 else x1
            dst = yout if l == depth - 1 else x1
            R0 = slot * S
            vb = l * NV
            chan_phase()
            A.reset(base_mark)
            cqnT = A.alloc(2 * S * 2).rearrange("p (c t) -> p c t", c=2)
            ckvnT = A.alloc(S * 2)
            kropeT = A.alloc(S * 2)
            pA_mark = A.mark()
            ufour = A.alloc(NS * 256 * 2).rearrange("p (s c) -> p s c", c=256)
            p1_mark = A.mark()
            wA = A.alloc(8 * 448 * 2).rearrange("p (k n) -> p k n", k=8)
            wF = A.alloc(8 * 256 * 2).rearrange("p (k n) -> p k n", k=8)
            wB = A.alloc(6 * 8 * 128 * 2).rearrange("p (c k n) -> p c k n", c=6, k=8)
            xb2 = [A.alloc(D * 4, F32) for _ in range(2)]
            xn = [A.alloc(D * 2) for _ in range(4)]
            xb = [xb2[i % 2] for i in range(4)]
            xkeys = [("xb", i % 2) for i in range(4)]
            nkeys = [("xn", i) for i in range(4)]
            hTs = [A.alloc(8 * TT * 2).rearrange("p (k t) -> p k t", k=8) for _ in range(2)]
            tnh = [A.alloc(TT * 4, F32) for _ in range(4)]
            stg = [A.alloc(2 * TT * 2).rearrange("p (c t) -> p c t", c=2) for _ in range(2)]
            tokst = [A.alloc(512 * 2) for _ in range(2)]
            rtmp = [A.alloc(64 * 4, F32) for _ in range(2)]
            pstb = [(bank[6].bitcast(BF16), ("bk", 6)), (bank[7].bitcast(BF16), ("bk", 7))]
            cw = newchan()
            cx = [newchan() for _ in range(2)]
            cx = [cx[i % 2] for i in range(4)]
            cst = [newchan() for _ in range(2)]
            dma(wA, wview(l, "WA", "(p k n) -> p k n", p=128, k=8), cw, ["wbf"], ["wA"])
            dma(wF, wview(l, "WF", "(p k n) -> p k n", p=128, k=8), cw, ["wbf"], ["wF"])
            dma(wB, wview(l, "WB", "(c p k n) -> p c k n", c=6, p=128, k=8), cw, ["wbf"], ["wB"])
            dma(gpost, gpost_d[l:l + 1, :].partition_broadcast(128), cw, [], ["gpost"])

            for tt_i in range(NT):
                hT = hTs[tt_i % 2]
                hk = ("hT", id(hT))
                make_hT_a(l, src, R0 + tt_i * TT, xb, xn, cx, xkeys, nkeys, 0)
                make_hT_b(l, xn, nkeys, hT, pstb)
                t0 = tt_i * TT
                def fm_pool(c, tt_i=tt_i, t0=t0, hT=hT, hk=hk):
                    st = stg[0]
                    ps = bank[3]
                    for k in range(8):
                        mm(ps, wB[:, c, k, :], hT[:, k, :], k == 0, k == 7, [hk, "wB"], [("bk", 3)])
                    cp("act", st[:, c, :], ps, [("bk", 3)], [("stg", 0)])
                    if c == 1:
                        dma(upool_d[:, :, t0:t0 + TT].rearrange("c p t -> p c t"), st, cst[0], [("stg", 0)],
                            [("upool_d", tt_i)])

                def fm_glu(c, tt_i=tt_i, t0=t0, hT=hT, hk=hk):
                    st = stg[1]
                    psb_, psa_ = bank[2], bank[3]
                    for k in range(8):
                        mm(psb_, wB[:, 4 + c, k, :], hT[:, k, :], k == 0, k == 7, [hk, "wB"], [("bk", 2)])
                    act(tnh[c], psb_, AF.Tanh, [("bk", 2)], [("tnh", c)], scale=0.5)
                    for k in range(8):
                        mm(psa_, wB[:, 2 + c, k, :], hT[:, k, :], k == 0, k == 7, [hk, "wB"], [("bk", 3)])
                    stt(st[:, c, :], tnh[c], 1.0, psa_, ALU.add, ALU.mult, [("tnh", c), ("bk", 3)], [("stg", 1)])
                    if c == 1:
                        dma(aconv_d[:, :, t0:t0 + TT].rearrange("c p t -> p c t"), st, cst[1], [("stg", 1)],
                            [("aconv_d", tt_i)])
                fm = [lambda: fm_pool(0), lambda: fm_pool(1), lambda: fm_glu(0), lambda: fm_glu(1)]
                for sub in range(4):
                    si = tt_i * 4 + sub
                    psA, psF = bank[sub % 2], bank[4 + sub % 2]
                    kA, kF = ("bk", sub % 2), ("bk", 4 + sub % 2)
                    tk = tokst[sub % 2]
                    for k in range(8):
                        mm(psA[:, 0:448], hT[:, k, sub * 128:(sub + 1) * 128], wA[:, k, :], k == 0, k == 7,
                           [hk, "wA"], [kA])
                    for k in range(8):
                        mm(psF[:, 0:256], hT[:, k, sub * 128:(sub + 1) * 128], wF[:, k, :], k == 0, k == 7,
                           [hk, "wF"], [kF])
                    cp("act", ufour[:, si, :], psF[:, 0:256], [kF], [("ufour", si)])
                    sq = small[:, 40 + 2 * (sub % 2):42 + 2 * (sub % 2)]
                    junk = tnh[0][:, 0:256]
                    act(junk, psA[:, 0:256], AF.Square, [kA], [("tnh", 0), ("sq", sub % 2)],
                        accum=sq[:, 0:1])
                    act(junk[:, 0:128], psA[:, 256:384], AF.Square, [kA], [("tnh", 0), ("sq", sub % 2)],
                        accum=sq[:, 1:2])
                    rs = small[:, 48 + 2 * (sub % 2):50 + 2 * (sub % 2)]
                    ts("pool", rs[:, 0:1], sq[:, 0:1], 1.0 / 256, ALU.mult, [("sq", sub % 2)], [("rs", sub % 2)],
                       s2=1e-6, op1=ALU.add)
                    ts("pool", rs[:, 1:2], sq[:, 1:2], 1.0 / 128, ALU.mult, [("sq", sub % 2)], [("rs", sub % 2)],
                       s2=1e-6, op1=ALU.add)
                    rq = small[:, 56 + 2 * (sub % 2):58 + 2 * (sub % 2)]
                    tt("pool", rq, rs, mhalf[:, 0:2], ALU.pow, [("rs", sub % 2), "mhalf"], [("rq", sub % 2)])
                    act(tk[:, 0:256], psA[:, 0:256], AF.Copy, [kA, ("rq", sub % 2)], [("tk", sub % 2)],
                        scale=rq[:, 0:1])
                    act(tk[:, 256:384], psA[:, 256:384], AF.Copy, [kA, ("rq", sub % 2)], [("tk", sub % 2)],
                        scale=rq[:, 1:2])
                    rt = rtmp[sub % 2]
                    cosk = ropetok[:, si * 32:(si + 1) * 32]
                    sink = ropetok[:, (NS + si) * 32:(NS + si + 1) * 32]
                    tt("dve", rt[:, 0:32], psA[:, 416:448], sink, ALU.mult, [kA, "ropetok"], [("rt", sub % 2)])
                    tt("dve", rt[:, 32:64], psA[:, 384:416], cosk, ALU.mult, [kA, "ropetok"],
                       [("rt2", sub % 2)])
                    tt("dve", tk[:, 384:416], rt[:, 0:32], rt[:, 32:64], ALU.add, [("rt", sub % 2), ("rt2", sub % 2)],
                       [("tk", sub % 2)])
                    fm[sub]()
                    pt, ptk = pstb[sub % 2]
                    for j in range(3):
                        tr(pt[:, j * 128:(j + 1) * 128], tk[:, j * 128:(j + 1) * 128], [("tk", sub % 2)],
                           [ptk])
                    tr(pt[0:32, 384:512], tk[:, 384:416], [("tk", sub % 2)], [ptk])
                    tsl = slice(si * 128, (si + 1) * 128)
                    tt("dve", cqnT[:, :, tsl], pt[:, 0:256].rearrange("p (c t) -> p c t", c=2),
                       vec[:, vb + V_GQ:vb + V_GQ + 2].unsqueeze(2).broadcast_to([128, 2, 128]), ALU.mult,
                       [ptk, "vec"], [("cqnT", si)])
                    ts("dve", ckvnT[:, tsl], pt[:, 256:384], vec[:, vb + V_GKV:vb + V_GKV + 1], ALU.mult,
                       [ptk, "vec"], [("ckvnT", si)])
                    cp("dve", kropeT[0:32, tsl], pt[0:32, 384:512], [ptk], [("kropeT", si)])
            P.barrier()
            chan_phase()
            A.reset(p1_mark)
            tabs = [A.alloc(2 * 4 * TT * 2).rearrange("p (a i t) -> p a i t", a=2, i=4) for _ in range(3)]
            fev = [A.alloc(TT * 2) for _ in range(4)]
            gev = [A.alloc(TT * 2) for _ in range(2)]
            ofst = A.alloc(2 * TT * 2).rearrange("p (c t) -> p c t", c=2)
            wfo = A.alloc(2 * 256 * 2).rearrange("p (k n) -> p k n", k=2)
            ctab = [newchan() for _ in range(6)]
            cof = newchan()
            dma(wfo, wview(l, "FOURW", "(p k n) -> p k n", p=128, k=2), cof, ["wbf"], ["wfo"])
            U = A.alloc(2 * 528 * 2).rearrange("p (c t) -> p c t", c=2)
            p1 = A.alloc(2 * 528 * 4, F32).rearrange("p (c t) -> p c t", c=2)
            p2 = A.alloc(2 * 528 * 4, F32).rearrange("p (c t) -> p c t", c=2)
            p3 = A.alloc(528 * 4, F32)
            p4 = A.alloc(528 * 4, F32)
            pinvE = A.alloc(2 * 16 * 4, F32).rearrange("p (c t) -> p c t", c=2)
            pinvM = vec[:, vb + V_PINV:vb + V_PINV + 2]
            acc = A.alloc(2 * TT * 4, F32).rearrange("p (c t) -> p c t", c=2)
            pdf = A.alloc(2 * TT * 2).rearrange("p (c t) -> p c t", c=2)
            Ac = A.alloc(2 * 544 * 2).rearrange("p (c t) -> p c t", c=2)
            zz = A.alloc(2 * TT * 4, F32).rearrange("p (c t) -> p c t", c=2)
            dd = A.alloc(2 * TT * 4, F32).rearrange("p (c t) -> p c t", c=2)
            ybf = A.alloc(2 * TT * 2).rearrange("p (c t) -> p c t", c=2)
            ysq = A.alloc(2 * TT * 2).rearrange("p (c t) -> p c t", c=2)
            mean = A.alloc(TT * 4, F32)
            var = A.alloc(TT * 4, F32)
            rstd = A.alloc(TT * 4, F32)
            sbf = A.alloc(2 * TT * 2).rearrange("p (c t) -> p c t", c=2)
            opst = A.alloc(4 * TT * 2).rearrange("p (c t) -> p c t", c=4)
            wpool = A.alloc(2 * 128 * 2).rearrange("p (c n) -> p c n", c=2)
            wcpw = A.alloc(2 * 256 * 2).rearrange("p (k n) -> p k n", k=2)
            tnhb = [A.alloc(TT * 4, F32) for _ in range(2)]
            cbi = [newchan() for _ in range(2)]
            cbo = [newchan() for _ in range(2)]
            dma(wpool, wview(l, "POOLW", "(p c n) -> p c n", p=128, c=2), cof, ["wbf"], ["wpool"])
            dma(wcpw, wview(l, "CONVPW", "(p k n) -> p k n", p=128, k=2), cof, ["wbf"], ["wcpw"])
            dma(pinvE[:, :, 0:8], pinv_d[:, :, 0:8], cof, [], ["pinvE"])
            dma(pinvE[:, :, 8:16], pinv_d[:, :, S - 8:S], cof, [], ["pinvE"])
            dg = A.alloc(62 * 128 * 2).rearrange("p (i n) -> p i n", i=62)
            for i_ in range(62):
                ts("dve", dg[:, i_, :], ident, vder[:, vb + V_CW + i_:vb + V_CW + i_ + 1], ALU.mult,
                   ["ident", "vder"], ["dg"])
            def branch_pc(tb):
                t0 = tb * TT
                lo, hi = max(t0 - 8, 0), min(t0 + TT + 8, S)
                if t0 == 0:
                    memset("pool", U[:, :, 0:8], 0.0, ["U"])
                if t0 + TT == S:
                    memset("pool", U[:, :, 520:528], 0.0, ["U"])
                dma(U[:, :, lo - (t0 - 8):hi - (t0 - 8)], upool_d[:, :, lo:hi].rearrange("c p t -> p c t"), cbi[0],
                    [("upool_d", i) for i in range(NT)], ["U"])
                lo, hi = max(t0 - 15, 0), min(t0 + TT + 15, S)
                if t0 == 0:
                    memset("pool", Ac[:, :, 0:15], 0.0, ["Ac"])
                if t0 + TT == S:
                    memset("pool", Ac[:, :, 527:544], 0.0, ["Ac"])
                dma(Ac[:, :, lo - (t0 - 15):hi - (t0 - 15)], aconv_d[:, :, lo:hi].rearrange("c p t -> p c t"), cbi[1],
                    [("aconv_d", i) for i in range(NT)], ["Ac"])
                tt("pool", p1[:, :, 0:527], U[:, :, 0:527], U[:, :, 1:528], ALU.add, ["U"], ["p1"])
                tt("pool", p2[:, :, 0:525], p1[:, :, 0:525], p1[:, :, 2:527], ALU.add, ["p1"], ["p2"])
                tt("pool", p3[:, 0:521], p2[:, 1, 0:521], p2[:, 1, 4:525], ALU.add, ["p2"], ["p3"])
                tt("pool", p4[:, 0:513], p3[:, 0:513], p3[:, 8:521], ALU.add, ["p3"], ["p4"])
                srcs = ((0, 0, p1[0:64, 0, 7:519]), (0, 64, p2[64:128, 0, 6:518]), (1, 0, p3[0:64, 4:516]),
                        (1, 64, p4[64:128, 0:512]))
                for (c, pb_, sap) in srcs:
                    ts("pool", acc[pb_:pb_ + 64, c, :], sap, pinvM[pb_:pb_ + 64, c:c + 1], ALU.mult,
                       ["p1", "p2", "p3", "p4", "vec"], [("accp", c, pb_)], s2=1.0, op1=ALU.mult)
                    if t0 == 0:
                        tt("pool", acc[pb_:pb_ + 64, c, 0:8], sap[:, 0:8], pinvE[pb_:pb_ + 64, c, 0:8], ALU.mult,
                           ["p1", "p2", "p3", "p4", "pinvE"], [("accp", c, pb_)])
                    if t0 + TT == S:
                        tt("pool", acc[pb_:pb_ + 64, c, 504:512], sap[:, 504:512], pinvE[pb_:pb_ + 64, c, 8:16],
                           ALU.mult, ["p1", "p2", "p3", "p4", "pinvE"], [("accp", c, pb_)])
                tt("pool", pdf, acc, U[:, :, 8:520], ALU.subtract, [("accp", c, pb_) for (c, pb_, _) in srcs] + ["U"],
                   ["pdf"])
                yield
                for c in range(2):
                    mm(bank[6], wpool[:, c, :], pdf[:, c, :], True, True, ["pdf", "wpool"], [("bk", 6)])
                    ts("dve", opst[:, c, :], bank[6], vec[:, vb + V_PSC + c:vb + V_PSC + c + 1], ALU.mult,
                       [("bk", 6), "vec"], [("opst", 0)])
                dma(obr_d[0:2, :, t0:t0 + TT].rearrange("c p t -> p c t"), opst[:, 0:2, :], cbo[0], [("opst", 0)],
                    [("obr_pool", tb)])
                for c in range(2):
                    for k in range(CONV_K):
                        mm(bank[6], dg[:, c * 31 + k, :], Ac[:, c, k:k + TT], k == 0, k == CONV_K - 1, ["Ac", "dg"],
                           [("bk", 6)])
                    ts("dve", zz[:, c, :], bank[6], vec[:, vb + V_CB + c:vb + V_CB + c + 1], ALU.add,
                       [("bk", 6), "vec"], [("zz", c)])
                for c in range(2):
                    cp("act", ybf[:, c, :], zz[:, c, :], [("zz", c)], [("ybf", c)])
                    act(ysq[:, c, :], zz[:, c, :], AF.Square, [("zz", c)], [("ysq", c)])
                yield
                for c in range(2):
                    mm(bank[6], ones, ybf[:, c, :], c == 0, c == 1, [("ybf", c), "ones"], [("bk", 6)])
                act(mean, bank[6], AF.Copy, [("bk", 6)], ["mean"], scale=1.0 / 256)
                for c in range(2):
                    mm(bank[6], ones, ysq[:, c, :], c == 0, c == 1, [("ysq", c), "ones"], [("bk", 6)])
                tt("pool", rstd, mean, mean, ALU.mult, ["mean"], ["rstd"])
                stt(var, bank[6], 1.0 / 256, rstd, ALU.mult, ALU.subtract, [("bk", 6), "rstd"], ["var"])
                act(var, var, AF.Ln, ["var", "epsc"], ["var"], bias=epsc[:, 0:1])
                act(rstd, var, AF.Exp, ["var"], ["rstd"], scale=-0.5)
                for c in range(2):
                    tt("dve", dd[:, c, :], zz[:, c, :], mean, ALU.subtract, [("zz", c), "mean"], [("dd", c)])
                    tt("dve", dd[:, c, :], dd[:, c, :], rstd, ALU.mult, [("dd", c), "rstd"], [("dd", c)])
                    act(tnhb[c], dd[:, c, :], AF.Tanh, [("dd", c), "vder"], [("tnhb", c)],
                        scale=vder[:, vb + V_LNG + c:vb + V_LNG + c + 1],
                        bias=vder[:, vb + V_LNB + c:vb + V_LNB + c + 1])
                    ts("dve", zz[:, c, :], dd[:, c, :], vec[:, vb + V_LNG + c:vb + V_LNG + c + 1], ALU.mult,
                       [("dd", c), "vec"], [("zz", c)], s2=vec[:, vb + V_LNB + c:vb + V_LNB + c + 1], op1=ALU.add)
                    stt(sbf[:, c, :], tnhb[c], 1.0, zz[:, c, :], ALU.add, ALU.mult, [("tnhb", c), ("zz", c)],
                        [("sbf", c)])
                yield
                for co in range(2):
                    for k in range(2):
                        mm(bank[6], wcpw[:, k, co * 128:(co + 1) * 128], sbf[:, k, :], k == 0, k == 1,
                           [("sbf", k), "wcpw"], [("bk", 6)])
                    ts("dve", opst[:, 2 + co, :], bank[6], 0.5, ALU.mult, [("bk", 6), "vec"], [("opst", 1)],
                       s2=vec[:, vb + V_PWB + co:vb + V_PWB + co + 1], op1=ALU.add)
                dma(obr_d[6:8, :, t0:t0 + TT].rearrange("c p t -> p c t"), opst[:, 2:4, :], cbo[1], [("opst", 1)],
                    [("obr_conv", tb)])

            gi = 0
            bgens = []
            dpend = [None]
            for j in range(NT):
                bgens.append(branch_pc(j))
                ngrp = NS // 4
                for ig in range(ngrp):
                    for st_ in range(4):
                        if ig == min(ngrp - 1, 2 * st_ + 1):
                            t_ = j - 3 + st_
                            if 0 <= t_ < NT:
                                next(bgens[t_], None)
                    tb = tabs[gi % 3]
                    for a in range(2):
                        dma(tb[:, a].rearrange("p i t -> p (i t)"), dft_d[a, j, ig], ctab[(gi % 3) * 2 + a], [],
                            [("tab", gi % 3, a)])
                    for ii in range(4):
                        i = ig * 4 + ii
                        for c in range(2):
                            for a in range(2):
                                mm(bank[a * 2 + c], ufour[:, i, c * 128:(c + 1) * 128], tb[:, a, ii, :],
                                   i == 0, i == NS - 1, [("tab", gi % 3, a)], [("bk", a * 2 + c)])
                    gi += 1
                    if ig == 1 and dpend[0] is not None:
                        dpend[0]()
                        dpend[0] = None
                for a in range(2):
                    for c in range(2):
                        act(fev[a * 2 + c], bank[a * 2 + c], AF.Copy, [("bk", a * 2 + c)], [("fev", a * 2 + c)],
                            scale=float((S * 64) ** -0.5))

                def post_j(j=j):
                    for c in range(2):
                        mm(bank[4 + c], c64, fev[c], True, False, [("fev", c), "cbd"], [("bk", 4 + c)])
                        mm(bank[4 + c], ns64, fev[2 + c], False, True, [("fev", 2 + c), "cbd"], [("bk", 4 + c)])
                        cp("dve", gev[c], bank[4 + c], [("bk", 4 + c)], [("gev", c)])
                    for co in range(2):
                        for k in range(2):
                            mm(bank[7], wfo[:, k, co * 128:(co + 1) * 128], gev[k], k == 0, k == 1,
                               [("gev", k), "wfo"], [("bk", 7)])
                        cp("dve", ofst[:, co, :], bank[7], [("bk", 7)], [("ofst", 0)])
                    dma(obr_d[8:10, :, j * TT:(j + 1) * TT].rearrange("c p t -> p c t"), ofst, cof, [("ofst", 0)],
                        [("obr_four", j)])
                dpend[0] = post_j
            if dpend[0] is not None:
                dpend[0]()
                dpend[0] = None
            for i_ in range(NT, NT + 3):
                for st_ in range(4):
                    t_ = i_ - 3 + st_
                    if 0 <= t_ < NT:
                        next(bgens[t_], None)
            P.barrier()
            chan_phase()
            A.reset(pA_mark)
            wuq = A.alloc(2 * 768 * 2).rearrange("p (k n) -> p k n", k=2)
            wuqs = A.alloc(2 * 768 * 2).rearrange("p (k n) -> p k n", k=2)
            wukv = A.alloc(1024 * 2)
            KT = [A.alloc(S * 2) for _ in range(2)]
            VT = [A.alloc(NS * 128 * 2).rearrange("p (s c) -> p s c", c=128) for _ in range(2)]
            QT = [A.alloc(S * 2) for _ in range(2)]
            pT = [A.alloc(TT * 2) for _ in range(4)]
            rtab = [A.alloc(2 * TT * 4, F32).rearrange("p (a t) -> p a t", a=2) for _ in range(2)]
            rr1 = [A.alloc(TT * 4, F32) for _ in range(2)]
            rr2 = [A.alloc(TT * 4, F32) for _ in range(2)]
            rec = [A.alloc(TT * 4, F32) for _ in range(2)]
            oat = [A.alloc(TT * 2) for _ in range(2)]
            cwa = newchan()
            crt = [newchan() for _ in range(2)]
            coa = [newchan() for _ in range(2)]
            dma(wuq, wview(l, "WUQ", "(p k n) -> p k n", p=128, k=2), cwa, ["wbf"], ["wuq"])
            dma(wuqs, wview(l, "WUQS", "(p k n) -> p k n", p=128, k=2), cwa, ["wbf"], ["wuqs"])
            dma(wukv, wview(l, "WUKV", "(p n) -> p n", p=128), cwa, ["wbf"], ["wukv"])
            memset("pool", VT[0][:, :, 64:128], 1.0, [("VT", 0)])
            memset("pool", VT[1][:, :, 0:64], 1.0, [("VT", 1)])
            for hb in range(2):
                for j in range(NT):
                    cp("dve", KT[hb][64:96, j * TT:(j + 1) * TT], kropeT[0:32, j * TT:(j + 1) * TT], [],
                       [("KT", hb)])
            rti = [0]

            def build_head(h):
                hb = h % 2
                for j in range(NT):
                    mm(bank[5][0:64, :], wukv[:, h * 128:h * 128 + 64], ckvnT[:, j * TT:(j + 1) * TT], True, True,
                       ["wukv"], [("bk", 5)])
                    cp("dve", KT[hb][0:64, j * TT:(j + 1) * TT], bank[5][0:64, :], [("bk", 5)], [("KT", hb)])
                for g in range(NS // 8):
                    for ii in range(8):
                        i = g * 8 + ii
                        mm(bank[6][:, ii * 64:(ii + 1) * 64], ckvnT[:, i * 128:(i + 1) * 128],
                           wukv[:, h * 128 + 64:h * 128 + 128], True, True, ["wukv"], [("bk", 6)])
                    vo = 0 if hb == 0 else 64
                    cp("dve", VT[hb][:, g * 8:(g + 1) * 8, vo:vo + 64],
                       bank[6].rearrange("p (i c) -> p i c", i=8), [("bk", 6)], [("VT", hb)])
                for j in range(NT):
                    rb = rti[0] % 2
                    rti[0] += 1
                    dma(rtab[rb][64:96, :, :], ropeT_d[:, :, j * TT:(j + 1) * TT].rearrange("a d t -> d a t"),
                        crt[rb], [], [("rtab", rb)])
                    psA, psB = bank[5], bank[7]
                    for k in range(2):
                        mm(psA[0:96, :], wuq[:, k, h * 96:(h + 1) * 96], cqnT[:, k, j * TT:(j + 1) * TT], k == 0,
                           k == 1, ["wuq"], [("bk", 5)])
                    for k in range(2):
                        mm(psB[0:96, :], wuqs[:, k, h * 96:(h + 1) * 96], cqnT[:, k, j * TT:(j + 1) * TT], k == 0,
                           k == 1, ["wuqs"], [("bk", 7)])
                    qs = slice(j * TT, (j + 1) * TT)
                    cp("dve", QT[hb][0:64, qs], psA[0:64, :], [("bk", 5)], [("QT", hb)])
                    tt("dve", rr1[rb][64:96, :], psB[64:96, :], rtab[rb][64:96, 1, :], ALU.mult,
                       [("bk", 7), ("rtab", rb)], [("rr1", rb)])
                    tt("dve", rr2[rb][64:96, :], psA[64:96, :], rtab[rb][64:96, 0, :], ALU.mult,
                       [("bk", 5), ("rtab", rb)], [("rr2", rb)])
                    tt("dve", QT[hb][64:96, qs], rr1[rb][64:96, :], rr2[rb][64:96, :], ALU.add,
                       [("rr1", rb), ("rr2", rb)], [("QT", hb)])

            build_head(0)
            qi = 0
            pi = 0
            for h in range(8):
                hb = h % 2
                tot = NT * NS

                def score(s_, hb=hb):
                    j_, kc_ = divmod(s_, NS)
                    sb = s_ % 3
                    mm(bank[sb], KT[hb][0:96, kc_ * 128:(kc_ + 1) * 128], QT[hb][0:96, j_ * TT:(j_ + 1) * TT], True,
                       True, [("KT", hb), ("QT", hb)], [("bk", sb)])
                score(0)
                score(1)
                for s_ in range(tot):
                    j, kc = divmod(s_, NS)
                    ob = 3 + (qi + j) % 2
                    ok = ("bk", ob)
                    qs = slice(j * TT, (j + 1) * TT)
                    if s_ + 2 < tot:
                        score(s_ + 2)
                    pb = pi % 4
                    pi += 1
                    act(pT[pb], bank[s_ % 3], AF.Exp, [("bk", s_ % 3)], [("pT", pb)], scale=SCALE)
                    mm(bank[ob], VT[hb][:, kc, :], pT[pb], kc == 0, kc == NS - 1, [("VT", hb), ("pT", pb)], [ok])
                    if kc == NS - 1:
                        d0, s0 = (0, 64) if hb == 0 else (64, 0)
                        rb = (qi + j) % 2
                        P.op("dve", (lambda o_, i_: (lambda e: e.reciprocal(out=o_, in_=i_)))(
                            rec[rb][d0:d0 + 64, :], bank[ob][s0:s0 + 64, :]), r=[ok], w=[("rec", rb)])
                        tt("dve", oat[rb][d0:d0 + 64, :], bank[ob][d0:d0 + 64, :], rec[rb][d0:d0 + 64, :], ALU.mult,
                           [ok, ("rec", rb)], [("oat", rb)])
                        dma(obr_d[2 + h // 2, d0:d0 + 64, qs], oat[rb][d0:d0 + 64, :], coa[rb], [("oat", rb)],
                            [("obr_attn", h // 2, j, hb)])
                        if j == NT // 2 - 1 and h + 1 < 8:
                            build_head(h + 1)
                qi += NT
            P.barrier()
            chan_phase()
            A.reset(base_mark)
            wout = A.alloc(8 * 1024 * 2).rearrange("p (k n) -> p k n", k=8)
            xbs = [[A.alloc(D * 4, F32) for _ in range(4)] for _ in range(2)]
            xns = [A.alloc(D * 2) for _ in range(4)]
            hTs = [A.alloc(8 * TT * 2).rearrange("p (k t) -> p k t", k=8) for _ in range(2)]
            Abufs = [A.alloc(10 * TT * 2).rearrange("p (c t) -> p c t", c=10) for _ in range(2)]
            oin = A.alloc(10 * TT * 2).rearrange("p (c t) -> p c t", c=10)
            mT = A.alloc(8 * TT * 2).rearrange("p (k t) -> p k t", k=8)
            wring = [A.alloc(4 * 1024 * 2).rearrange("p (c k n) -> p c k n", c=4, k=8) for _ in range(3)]
            wupr = [A.alloc(10 * 128 * 2).rearrange("p (k n) -> p k n", k=10) for _ in range(2)]
            tnh = [A.alloc(TT * 4, F32) for _ in range(4)]
            tmp = [A.alloc(TT * 4, F32) for _ in range(4)]
            s01 = A.alloc(TT * 4, F32)
            s23 = A.alloc(TT * 4, F32)
            otmp = [A.alloc(D * 4, F32) for _ in range(2)]
            pstb = [(bank[6].bitcast(BF16), ("bk", 6)), (bank[7].bitcast(BF16), ("bk", 7))]
            cw2 = newchan()
            cxs = [[newchan() for _ in range(4)] for _ in range(2)]
            cwr = [newchan() for _ in range(3)]
            cwu = [newchan() for _ in range(2)]
            cin = newchan()
            cout = [newchan() for _ in range(2)]
            dma(wout, wview(l, "WOUT", "(p k n) -> p k n", p=128, k=8), cw2, ["wbf"], ["wout"])
            WGv = wview(l, "WG", "(c p k n) -> p c k n", c=10, p=128, k=8)
            WMv = wview(l, "WM", "(j i p k n) -> j p i k n", j=8, i=4, p=128, k=8)
            WUPv = wview(l, "WUP", "(j p k n) -> j p k n", j=8, p=128, k=10)
            st2 = dict(wri=0, gbank=0, oi=0)
            nkeys = [("xn", i) for i in range(4)]

            def prep_a(tt_i):
                pb_ = tt_i % 2
                make_hT_a(l, src, R0 + tt_i * TT, xbs[pb_], xns, cxs[pb_], [("xb", pb_, i) for i in range(4)], nkeys,
                          16 * pb_)

            def prep_b(tt_i):
                pb_ = tt_i % 2
                t0 = tt_i * TT
                hT = hTs[pb_]
                hk = ("hT", id(hT))
                Abuf = Abufs[pb_]
                make_hT_b(l, xns, nkeys, hT, pstb)
                dma(oin, obr_d[:, :, t0:t0 + TT].rearrange("c p t -> p c t"), cin, [], ["oin"])
                for g3, (c0, c1) in enumerate(((0, 4), (4, 8), (8, 10))):
                    wri = st2["wri"]
                    wr = wring[wri % 3]
                    wk = ("wring", wri % 3)
                    dma(wr[:, 0:c1 - c0], WGv[:, c0:c1], cwr[wri % 3], ["wbf"], [wk])
                    for c in range(c0, c1):
                        gb = st2["gbank"]
                        ps = bank[gb % 2]
                        pk = ("bk", gb % 2)
                        tb = gb % 4
                        st2["gbank"] += 1
                        for k in range(8):
                            mm(ps, wr[:, c - c0, k, :], hT[:, k, :], k == 0, k == 7, [hk, wk], [pk])
                        act(tnh[tb], ps, AF.Tanh, [pk], [("tnh", tb)], scale=0.5)
                        stt(Abuf[:, c, :], tnh[tb], 1.0, ps, ALU.add, ALU.mult, [("tnh", tb), pk], [("A", pb_, c)])
                        stt(Abuf[:, c, :], oin[:, c, :], 0.25, Abuf[:, c, :], ALU.mult, ALU.mult,
                            ["oin", ("A", pb_, c)], [("A", pb_, c)])
                    st2["wri"] += 1

            def merge(tt_i, mid=None):
                pb_ = tt_i % 2
                t0 = tt_i * TT
                hT = hTs[pb_]
                hk = ("hT", id(hT))
                Abuf = Abufs[pb_]
                xb = xbs[pb_]
                kk0 = (0, 2, 6, 8)
                nk = (2, 4, 2, 2)
                for j in range(8):
                    wri = st2["wri"]
                    wr = wring[wri % 3]
                    wk = ("wring", wri % 3)
                    wu = wupr[j % 2]
                    uk = ("wupr", j % 2)
                    if j == 0 and st2.get("pref") == wri:
                        pass
                    else:
                        dma(wr, WMv[j], cwr[wri % 3], ["wbf"], [wk])
                        dma(wu, WUPv[j], cwu[j % 2], ["wbf"], [uk])
                    for i in range(4):
                        gb = st2["gbank"]
                        ps = bank[gb % 2]
                        pk = ("bk", gb % 2)
                        tb = gb % 4
                        st2["gbank"] += 1
                        for k in range(8):
                            mm(ps, wr[:, i, k, :], hT[:, k, :], k == 0, k == 7, [hk, wk], [pk])
                        act(tnh[tb], ps, AF.Tanh, [pk, "vder"], [("tnh", tb)], scale=0.5,
                            bias=vder[:, vb + V_GB + i * 8 + j:vb + V_GB + i * 8 + j + 1])
                        pu = bank[2 + i % 2]
                        pku = ("bk", 2 + i % 2)
                        for k in range(nk[i]):
                            mm(pu, wu[:, kk0[i] + k, :], Abuf[:, kk0[i] + k, :], k == 0, k == nk[i] - 1,
                               [uk, ("A", pb_, kk0[i] + k)], [pku])
                        stt(tmp[i], tnh[tb], 1.0, pu, ALU.add, ALU.mult, [("tnh", tb), pku], [("tmp", i)])
                    tt("pool", s01, tmp[0], tmp[1], ALU.add, [("tmp", 0), ("tmp", 1)], ["s01"])
                    tt("pool", s23, tmp[2], tmp[3], ALU.add, [("tmp", 2), ("tmp", 3)], ["s23"])
                    tt("pool", mT[:, j, :], s01, s23, ALU.add, ["s01", "s23"], [("mT", j)])
                    st2["wri"] += 1
                    if j == 3 and mid is not None:
                        mid()
                if tt_i + 1 < NT:
                    wri = st2["wri"]
                    dma(wring[wri % 3], WMv[0], cwr[wri % 3], ["wbf"], [("wring", wri % 3)])
                    dma(wupr[0], WUPv[0], cwu[0], ["wbf"], [("wupr", 0)])
                    st2["pref"] = wri
                for sub in range(4):
                    pp = 4 if sub % 2 == 0 else 2
                    po = psum_t[:, pp * 512:(pp + 2) * 512]
                    kx = ("xb", pb_, sub)
                    for n in range(2):
                        for k in range(8):
                            mm(bank[pp + n], mT[:, k, sub * 128:(sub + 1) * 128], wout[:, k, n * 512:(n + 1) * 512],
                               k == 0, k == 7, [("mT", k), "wout"], [("bk", pp + n)])
                    so = 32 + 8 * (sub % 2)
                    sq = small[:, so:so + 2]
                    for n in range(2):
                        act(tnh[n][:, 0:512], bank[pp + n], AF.Square, [("bk", pp + n)], [("tnh", n), ("osq", sub % 2, n)],
                            accum=sq[:, n:n + 1])
                    tt("pool", small[:, so + 2:so + 3], sq[:, 0:1], sq[:, 1:2], ALU.add,
                       [("osq", sub % 2, 0), ("osq", sub % 2, 1)], [("os1", sub % 2)])
                    ts("pool", small[:, so + 3:so + 4], small[:, so + 2:so + 3], 1.0 / D, ALU.mult, [("os1", sub % 2)],
                       [("os2", sub % 2)], s2=1e-6, op1=ALU.add)
                    tt("pool", small[:, so + 4:so + 5], small[:, so + 3:so + 4], mhalf[:, 0:1], ALU.pow,
                       [("os2", sub % 2), "mhalf"], [("os3", sub % 2)])
                    ot = otmp[sub % 2]
                    stt(ot, po, small[:, so + 4:so + 5], gpost, ALU.mult, ALU.mult,
                        [("bk", pp), ("bk", pp + 1), ("os3", sub % 2), "gpost"], [("otmp", sub % 2)])
                    tt("dve", xb[sub], ot, xb[sub], ALU.add, [("otmp", sub % 2), kx], [kx])
                    dma(dst[R0 + t0 + sub * 128:R0 + t0 + (sub + 1) * 128, :], xb[sub], cout[sub % 2], [kx],
                        [("dst", slot, l)])

            prep_a(0)
            prep_b(0)
            for tt_i in range(NT):
                if tt_i + 1 < NT:
                    prep_a(tt_i + 1)
                    merge(tt_i, (lambda t=tt_i + 1: prep_b(t)))
                else:
                    merge(tt_i)
            P.barrier()

    P.finalize()
    with nc.Block() as block:
        @block.tensor
        def _(e):
            P.replay("pe", e, sems)

        @block.scalar
        def _(e):
            P.replay("act", e, sems)

        @block.vector
        def _(e):
            P.replay("dve", e, sems)

        @block.gpsimd
        def _(e):
            P.replay("pool", e, sems)

        @block.sync
        def _(e):
            P.replay("sp", e, sems)
    es.close()
    return nc


def _pack_layer(l, w):
    f = np.float32
    out = np.zeros(W_LAYER, f)

    def put(name, arr):
        a = np.ascontiguousarray(arr, dtype=f).reshape(-1)
        assert a.size == dict(W_SIZES)[name], (name, a.size)
        out[W_OFF[name]:W_OFF[name] + a.size] = a
    w_in = w["w_in"][l]
    wk = w_in.reshape(8, 128, NIN)
    colsA = np.concatenate([np.arange(OFF_CQ, OFF_CQ + 256), np.arange(OFF_KVA, OFF_KVA + 160),
                            np.arange(OFF_KVA + 144, OFF_KVA + 160), np.arange(OFF_KVA + 128, OFF_KVA + 144)])
    put("WA", wk[:, :, colsA].transpose(1, 0, 2))
    put("WF", wk[:, :, OFF_UFOUR:OFF_UFOUR + 256].transpose(1, 0, 2))
    colsB = np.concatenate([np.arange(OFF_UPOOL, OFF_UPOOL + 256), np.arange(OFF_UCONV, OFF_UCONV + 512)])
    put("WB", wk[:, :, colsB].reshape(8, 128, 6, 128).transpose(2, 1, 0, 3))
    put("WG", wk[:, :, OFF_GPOOL:OFF_GPOOL + 1280].reshape(8, 128, 10, 128).transpose(2, 1, 0, 3))
    put("WM", wk[:, :, OFF_MERGE:].reshape(8, 128, 4, 8, 128).transpose(3, 2, 1, 0, 4))
    up = np.concatenate([w["w_up_pool"][l], w["w_up_attn"][l], w["w_up_conv"][l], w["w_up_fourier"][l]], 0)
    put("WUP", up.reshape(10, 128, 8, 128).transpose(2, 1, 0, 3))
    put("WOUT", w["w_out"][l].reshape(8, 128, 1024).transpose(1, 0, 2))
    uq = w["w_uq"][l]
    put("WUQ", uq.reshape(2, 128, 768).transpose(1, 0, 2))
    uqs = uq.reshape(256, 8, 96).copy()
    uqs[:, :, 64:80], uqs[:, :, 80:96] = uq.reshape(256, 8, 96)[:, :, 80:96], uq.reshape(256, 8, 96)[:, :, 64:80]
    put("WUQS", uqs.reshape(2, 128, 768).transpose(1, 0, 2))
    put("WUKV", w["w_ukv"][l])
    pw = np.zeros((128, 2, 128), f)
    for c in range(2):
        pw[0:64, c, 0:64] = w["pool_w"][l][2 * c]
        pw[64:128, c, 64:128] = w["pool_w"][l][2 * c + 1]
    put("POOLW", pw)
    put("CONVPW", w["conv_pw_w"][l].reshape(2, 128, 256).transpose(1, 0, 2))
    put("FOURW", w["fourier_w"][l].reshape(2, 128, 256).transpose(1, 0, 2))
    return out


def _pack_vecs(l, w):
    v = np.zeros((128, NV), np.float32)
    v[:, V_GPRE:V_GPRE + 8] = w["pre_norm_g"][l].reshape(8, 128).T
    v[:, V_GQ:V_GQ + 2] = w["q_norm_g"][l].reshape(2, 128).T
    v[:, V_GKV] = w["kv_norm_g"][l]
    v[:, V_GB:V_GB + 32] = w["gate_b"][l].reshape(32, 128).T
    v[:, V_PSC:V_PSC + 2] = w["pool_scale"][l].reshape(2, 128).T
    v[:, V_CW:V_CW + 62] = w["conv_w"][l].reshape(31, 2, 128).transpose(2, 1, 0).reshape(128, 62)
    v[:, V_CB:V_CB + 2] = w["conv_b"][l].reshape(2, 128).T
    v[:, V_LNG:V_LNG + 2] = w["conv_ln_g"][l].reshape(2, 128).T
    v[:, V_LNB:V_LNB + 2] = w["conv_ln_b"][l].reshape(2, 128).T
    v[:, V_PWB:V_PWB + 2] = w["conv_pw_b"][l].reshape(2, 128).T
    for c in range(2):
        v[0:64, V_PINV + c] = 1.0 / (2, 4, 8, 16)[2 * c]
        v[64:128, V_PINV + c] = 1.0 / (2, 4, 8, 16)[2 * c + 1]
    return v


_CONST_CACHE = {}


def _consts(S):
    if S in _CONST_CACHE:
        return _CONST_CACHE[S]
    NS = S // 128
    f = np.float32
    inv_freq = (1.0 / (f(10000.0) ** (np.arange(0, 32, 2, dtype=f) / f(32)))).astype(f)
    ang = (np.arange(S, dtype=f)[:, None] * inv_freq[None, :]).astype(f)
    ang = np.concatenate([ang, ang], -1)
    cos = np.cos(ang).astype(f)
    sin = np.sin(ang).astype(f)
    sinS = sin.copy()
    sinS[:, 0:16] *= -1
    ropetok = np.concatenate([cos.reshape(NS, 128, 32).transpose(1, 0, 2).reshape(128, NS * 32),
                              sinS.reshape(NS, 128, 32).transpose(1, 0, 2).reshape(128, NS * 32)], 1)
    ropeT = np.stack([cos.T, sinS.T], 0).astype(f)
    idx = (np.arange(S, dtype=np.int64)[:, None] * np.arange(S, dtype=np.int64)[None, :]) % S
    angd = idx.astype(np.float64) * (2 * np.pi / S)
    dft = np.stack([np.cos(angd).astype(NPBF), np.sin(angd).astype(NPBF)], 0)
    dft = np.ascontiguousarray(dft.reshape(2, S // 512, 4, 128, S // TT, TT).transpose(0, 4, 1, 3, 2, 5)).reshape(
        2, S // TT, S // 512, 128, 4 * TT)
    a64 = (np.arange(64)[:, None] * np.arange(64)[None, :]) % 64 * (2 * np.pi / 64)
    cbd = np.zeros((128, 512), np.float64)
    for b in range(2):
        cbd[b * 64:(b + 1) * 64, b * 64:(b + 1) * 64] = np.cos(a64)
        cbd[b * 64:(b + 1) * 64, 128 + b * 64:128 + (b + 1) * 64] = -np.sin(a64)
    cbd[:, 256:384] = 1.0
    cbd[:, 384:512] = np.eye(128)
    t = np.arange(S)
    pinv = np.zeros((128, 2, S), f)
    for g, wdw in enumerate((2, 4, 8, 16)):
        lo = np.clip(t - wdw // 2, 0, S)
        hi = np.clip(t + wdw // 2, 0, S)
        pinv[(g % 2) * 64:(g % 2) * 64 + 64, g // 2, :] = (1.0 / (hi - lo).astype(f))[None, :]
    res = dict(ropetok=np.ascontiguousarray(ropetok, dtype=f), ropeT=ropeT, dft=dft, cbd=cbd.astype(NPBF), pinv=pinv)
    _CONST_CACHE[S] = res
    return res


_NC_CACHE = {}


def run(xseqs, w, S, ncores, nslot):
    key = (S, nslot)
    if key not in _NC_CACHE:
        _NC_CACHE[key] = build(S, nslot)
    nc = _NC_CACHE[key]
    wp = np.concatenate([_pack_layer(l, w) for l in range(DEPTH)])
    vv = np.concatenate([_pack_vecs(l, w) for l in range(DEPTH)], 1)
    cst = _consts(S)
    gpost = np.ascontiguousarray(w["post_norm_g"], dtype=np.float32)
    in_maps = []
    for c in range(ncores):
        m = dict(xin=np.ascontiguousarray(xseqs[c * nslot:(c + 1) * nslot].reshape(nslot * S, D)), wpack=wp, vecs=vv,
                 gpost=gpost, ropetok=cst["ropetok"], ropeT=cst["ropeT"], dft=cst["dft"], cbd=cst["cbd"],
                 pinv=cst["pinv"])
        in_maps.append(m)
    res = run_bass_kernel_spmd(nc, in_maps, core_ids=list(range(ncores)))
    return np.stack([np.asarray(r["yout"]).reshape(nslot, S, D) for r in res.results], 0).reshape(
        ncores * nslot, S, D)


def kernel(**inputs):
    w = {k: np.asarray(v, dtype=np.float32) for k, v in inputs.items() if k not in ("x_prompt", "x_sample")}
    xp = np.asarray(inputs["x_prompt"], dtype=np.float32)
    xs = np.asarray(inputs["x_sample"], dtype=np.float32)
    S = xp.shape[1]
    seqs = np.concatenate([xp, xs], 0)
    nseq = seqs.shape[0]
    nslot = 3
    order = list(range(nseq)) + [0] * (NCORES * nslot - nseq)
    xin = seqs[order]
    y = run(xin, w, S, NCORES, nslot)
    y = y[:nseq]
    return (np.ascontiguousarray(y[:xp.shape[0]]), np.ascontiguousarray(y[xp.shape[0]:]))
```

```python
import numpy as np
import ml_dtypes
from contextlib import ExitStack
import concourse.bass as bass
import concourse.mybir as mybir
from concourse.bass_utils import run_bass_kernel_spmd

F32 = mybir.dt.float32
BF16 = mybir.dt.bfloat16
AF = mybir.ActivationFunctionType
ALU = mybir.AluOpType
NPBF = ml_dtypes.bfloat16

D = 1024
NIN = 6816
DEPTH = 2
NCORES = 8
TT = 512
CONV_K = 31
OFF_UPOOL, OFF_CQ, OFF_KVA, OFF_UCONV, OFF_UFOUR = 0, 256, 512, 672, 1184
OFF_GPOOL, OFF_GATTN, OFF_GCONV, OFF_GFOUR, OFF_MERGE = 1440, 1696, 2208, 2464, 2720
SCALE = 96.0 ** -0.5

W_SIZES = [("WA", 128 * 8 * 448), ("WF", 128 * 8 * 256), ("WB", 6 * 128 * 8 * 128), ("WG", 10 * 128 * 8 * 128),
           ("WM", 32 * 128 * 8 * 128), ("WUP", 8 * 128 * 10 * 128), ("WOUT", 128 * 8 * 1024),
           ("WUQ", 128 * 2 * 768), ("WUQS", 128 * 2 * 768), ("WUKV", 128 * 1024), ("POOLW", 128 * 2 * 128),
           ("CONVPW", 128 * 2 * 256), ("FOURW", 128 * 2 * 256)]
W_OFF = {}
_o = 0
for _n, _s in W_SIZES:
    W_OFF[_n] = _o
    _o += _s
W_LAYER = ((_o + 2047) // 2048) * 2048
V_GPRE, V_GQ, V_GKV, V_GB, V_PSC, V_CW, V_CB, V_LNG, V_LNB, V_PWB = 0, 8, 10, 11, 43, 45, 107, 109, 111, 113
V_PINV = 116
NV = 118


class Chan:
    def __init__(self, sem):
        self.sem = sem
        self.cnt = 0
        self.last = None


class Op:
    __slots__ = ("eng", "fn", "deps", "is_dma", "chan", "val", "sig", "need")


class Prog:
    ENG = ("pe", "act", "dve", "pool", "sp")

    def __init__(self):
        self.q = {e: [] for e in self.ENG}
        self.lastw = {}
        self.rd = {}
        self.chans = []
        self.lastreal = {}

    def chan(self, sem):
        c = Chan(sem)
        self.chans.append(c)
        return c

    def op(self, eng, fn, r=(), w=(), chan=None):
        o = Op()
        o.eng, o.fn, o.is_dma, o.chan, o.sig, o.need, o.val = eng, fn, chan is not None, chan, 0, False, 0
        deps = {}

        def add(d, raw):
            if d is None:
                return
            if d.is_dma or o.is_dma or d.eng != eng or raw or eng != "pe":
                deps[id(d)] = d
        for k in r:
            add(self.lastw.get(k), True)
        for k in w:
            add(self.lastw.get(k), False)
            rr = self.rd.get(k)
            if rr:
                for d in rr.values():
                    add(d, False)
        if chan is not None:
            if chan.last is not None:
                deps[id(chan.last)] = chan.last
            chan.cnt += 16
            o.val = chan.cnt
            chan.last = o
        o.deps = list(deps.values())
        for k in r:
            rr = self.rd.get(k)
            if rr is None:
                rr = self.rd[k] = {}
            rr[("d", id(o)) if o.is_dma else eng] = o
        for k in w:
            self.lastw[k] = o
            self.rd[k] = {}
        self.q[eng].append(o)
        if not o.is_dma:
            self.lastreal[eng] = o
        return o

    def barrier(self):
        deps = [d for d in self.lastreal.values()] + [c.last for c in self.chans if c.last is not None]
        for e in self.ENG:
            o = Op()
            o.eng, o.fn, o.is_dma, o.chan, o.sig, o.need, o.val = e, None, False, None, 0, False, 0
            o.deps = [d for d in deps if d.is_dma or d.eng != e]
            self.q[e].append(o)
        self.lastw.clear()
        self.rd.clear()

    def finalize(self):
        for e in self.ENG:
            for o in self.q[e]:
                for d in o.deps:
                    d.need = True
        for e in self.ENG:
            c = 0
            for o in self.q[e]:
                if o.is_dma or o.fn is None:
                    continue
                if o.need:
                    c += 1
                    o.sig = c

    def replay(self, ename, e, sems):
        waited = {}
        for o in self.q[ename]:
            need = {}
            for d in o.deps:
                if d.is_dma:
                    sem, v = d.chan.sem, d.val
                else:
                    sem, v = sems[d.eng], d.sig
                k = id(sem)
                if waited.get(k, 0) >= v:
                    continue
                if k not in need or need[k][1] < v:
                    need[k] = (sem, v)
            for k, (sem, v) in need.items():
                e.wait_ge(sem, v)
                waited[k] = v
            if o.fn is None:
                continue
            ins = o.fn(e)
            if o.is_dma:
                ins.then_inc(o.chan.sem, 16)
            elif o.need:
                ins.then_inc(sems[o.eng], 1)


class Arena:
    def __init__(self, t, nbytes):
        self.t = t
        self.n = nbytes
        self.o = 0

    def alloc(self, nbytes, dtype=BF16):
        req = nbytes
        nbytes = (nbytes + 31) // 32 * 32
        assert self.o + nbytes <= self.n, f"arena overflow {self.o}+{nbytes}>{self.n}"
        ap = self.t[:, self.o // 2:(self.o + req) // 2]
        self.o += nbytes
        if dtype == F32:
            ap = ap.bitcast(F32)
        return ap

    def mark(self):
        return self.o

    def reset(self, m):
        self.o = m


def build(S, NSLOT, depth=DEPTH, dbg=False):
    NT = S // TT
    NS = S // 128
    nc = bass.Bass("TRN2", target_bir_lowering=False)
    P = Prog()
    es = ExitStack()
    xin = nc.dram_tensor("xin", [NSLOT * S, D], F32, kind="ExternalInput").ap()
    wpack = nc.dram_tensor("wpack", [DEPTH * W_LAYER], F32, kind="ExternalInput").ap()
    vecs = nc.dram_tensor("vecs", [128, DEPTH * NV], F32, kind="ExternalInput").ap()
    gpost_d = nc.dram_tensor("gpost", [DEPTH, D], F32, kind="ExternalInput").ap()
    ropetok_d = nc.dram_tensor("ropetok", [128, 2 * NS * 32], F32, kind="ExternalInput").ap()
    ropeT_d = nc.dram_tensor("ropeT", [2, 32, S], F32, kind="ExternalInput").ap()
    dft_d = nc.dram_tensor("dft", [2, S // TT, S // 512, 128, 4 * TT], BF16, kind="ExternalInput").ap()
    cbd_d = nc.dram_tensor("cbd", [128, 4 * 128], BF16, kind="ExternalInput").ap()
    pinv_d = nc.dram_tensor("pinv", [128, 2, S], F32, kind="ExternalInput").ap()
    yout = nc.dram_tensor("yout", [NSLOT * S, D], F32, kind="ExternalOutput").ap()
    wbf = nc.dram_tensor("wbf", [DEPTH * W_LAYER], BF16, kind="Internal").ap()
    x1 = nc.dram_tensor("x1", [NSLOT * S, D], F32, kind="Internal").ap()
    sk = "ExternalOutput" if dbg else "Internal"
    upool_d = nc.dram_tensor("upool_s", [2, 128, S], BF16, kind=sk).ap()
    aconv_d = nc.dram_tensor("aconv_s", [2, 128, S], BF16, kind=sk).ap()
    obr_d = nc.dram_tensor("obr_s", [10, 128, S], BF16, kind=sk).ap()

    ARENA_BYTES = 188 * 1024
    arena_t = es.enter_context(nc.sbuf_tensor("arena", [128, ARENA_BYTES // 2], BF16))
    psum_t = es.enter_context(nc.psum_tensor("psum", [128, 4096], F32))
    A = Arena(arena_t, ARENA_BYTES)
    bank = [psum_t[:, b * 512:(b + 1) * 512] for b in range(8)]
    sems = {e: es.enter_context(nc.semaphore("s_" + e)) for e in ("pe", "act", "dve", "pool")}
    nchan = [0]

    def newchan_raw():
        nchan[0] += 1
        return P.chan(es.enter_context(nc.semaphore("ch%d" % nchan[0])))
    chpool = [newchan_raw() for _ in range(20)]
    chidx = [0]

    def newchan():
        c = chpool[chidx[0] % len(chpool)]
        chidx[0] += 1
        return c

    def chan_phase():
        chidx[0] = 0

    def mm(out, lhsT, rhs, st, sp, r, w):
        P.op("pe", lambda e: e.matmul(out, lhsT=lhsT, rhs=rhs, start=st, stop=sp), r=r, w=w)

    def tr(out, in_, r, w):
        P.op("pe", lambda e: e.transpose(out=out, in_=in_, identity=ident), r=list(r) + ["ident"], w=w)

    def act(out, in_, func, r, w, scale=1.0, bias=None, accum=None):
        kw = {}
        if bias is not None:
            kw["bias"] = bias
        if accum is not None:
            kw["accum_out"] = accum
        P.op("act", lambda e: e.activation(out=out, in_=in_, func=func, scale=scale, **kw), r=r, w=w)

    def tt(eng, out, a, b, op, r, w):
        P.op(eng, lambda e: e.tensor_tensor(out=out, in0=a, in1=b, op=op), r=r, w=w)

    def ts(eng, out, a, s1, op0, r, w, s2=None, op1=None):
        if op1 is None:
            P.op(eng, lambda e: e.tensor_scalar(out=out, in0=a, scalar1=s1, scalar2=None, op0=op0), r=r, w=w)
        else:
            P.op(eng, lambda e: e.tensor_scalar(out=out, in0=a, scalar1=s1, scalar2=s2, op0=op0, op1=op1), r=r, w=w)

    def stt(out, in0, scalar, in1, op0, op1, r, w):
        P.op("dve", lambda e: e.scalar_tensor_tensor(out=out, in0=in0, scalar=scalar, in1=in1, op0=op0, op1=op1),
             r=r, w=w)

    def cp(eng, out, in_, r, w):
        if eng == "act":
            act(out, in_, AF.Copy, r, w)
        else:
            P.op(eng, lambda e: e.tensor_copy(out=out, in_=in_), r=r, w=w)

    def dma(out, in_, chan, r, w, eng="sp"):
        P.op(eng, lambda e: e.dma_start(out=out, in_=in_), r=r, w=w, chan=chan)

    def memset(eng, ap, val, w):
        P.op(eng, lambda e: e.memset(ap, val), r=(), w=w)

    ident = A.alloc(256)
    ones = A.alloc(256)
    cbd = A.alloc(4 * 256)
    c64 = cbd[:, 0:128]
    ns64 = cbd[:, 128:256]
    vec = A.alloc(DEPTH * NV * 4, F32)
    vder = A.alloc(DEPTH * NV * 4, F32)
    ropetok = A.alloc(2 * NS * 32 * 4, F32)
    mhalf = A.alloc(TT * 4, F32)
    gpost = A.alloc(D * 4, F32)
    small = A.alloc(96 * 4, F32)
    epsc = A.alloc(32, F32)
    c_const = newchan_raw()
    dma(ident, cbd_d[:, 384:512], c_const, [], ["ident"])
    dma(ones, cbd_d[:, 256:384], c_const, [], ["ones"])
    dma(cbd[:, 0:256], cbd_d[:, 0:256], c_const, [], ["cbd"])
    dma(vec, vecs[:, :], c_const, [], ["vec"])
    dma(ropetok, ropetok_d[:, :], c_const, [], ["ropetok"])
    memset("pool", mhalf, -0.5, ["mhalf"])
    memset("pool", epsc, 1e-5, ["epsc"])
    for l in range(DEPTH):
        b = l * NV
        for (o0, o1, f) in ((V_GB, V_GB + 32, 0.5), (V_CW, V_CW + 62, 0.5),
                            (V_LNG, V_LNG + 2, 0.5), (V_LNB, V_LNB + 2, 0.5)):
            ts("pool", vder[:, b + o0:b + o1], vec[:, b + o0:b + o1], f, ALU.mult, ["vec"], ["vder"])

    c_cast = newchan_raw()
    ROWS = DEPTH * W_LAYER // 2048
    wsrc = wpack.rearrange("(r c) -> r c", c=2048)
    wdst = wbf.rearrange("(r c) -> r c", c=2048)
    r0 = 0
    while r0 < ROWS:
        r1 = min(ROWS, r0 + 2048)
        dma(wdst[r0:r1, :], wsrc[r0:r1, :], c_cast, [], ["wbf"], eng="pool")
        r0 = r1
    P.barrier()
    base_mark = A.mark()

    def wview(l, name, pattern, **kw):
        o = l * W_LAYER + W_OFF[name]
        n = dict(W_SIZES)[name]
        return wbf[o:o + n].rearrange(pattern, **kw)

    def make_hT_a(l, src, row0, xb, xn, cx, xkeys, nkeys, so):
        for sub in range(4):
            kx = xkeys[sub]
            dma(xb[sub], src[row0 + sub * 128: row0 + (sub + 1) * 128, :], cx[sub], [], [kx])
            ssq = small[:, so + sub:so + sub + 1]
            act(xn[sub], xb[sub], AF.Square, [kx], [nkeys[sub], ("ssq", so, sub)], accum=ssq)
            ts("pool", small[:, so + 4 + sub:so + 5 + sub], ssq, 1.0 / D, ALU.mult, [("ssq", so, sub)],
               [("ssq2", so, sub)], s2=1e-6, op1=ALU.add)
            tt("pool", small[:, so + 8 + sub:so + 9 + sub], small[:, so + 4 + sub:so + 5 + sub], mhalf[:, 0:1],
               ALU.pow, [("ssq2", so, sub), "mhalf"], [("rstd", so, sub)])
            act(xn[sub], xb[sub], AF.Copy, [kx, ("rstd", so, sub)], [nkeys[sub]],
                scale=small[:, so + 8 + sub:so + 9 + sub])

    def make_hT_b(l, xn, nkeys, hT, pstb):
        vb = l * NV
        for sub in range(4):
            pt, ptk = pstb[sub % 2]
            for k in range(8):
                tr(pt[:, k * 128:(k + 1) * 128], xn[sub][:, k * 128:(k + 1) * 128], [nkeys[sub]], [ptk])
            tt("dve", hT[:, :, sub * 128:(sub + 1) * 128], pt.rearrange("p (k t) -> p k t", k=8),
               vec[:, vb + V_GPRE:vb + V_GPRE + 8].unsqueeze(2).broadcast_to([128, 8, 128]), ALU.mult,
               [ptk, "vec"], [("hT", id(hT))])

    for slot in range(NSLOT):
        for l in range(depth):
            src = xin if l == 0 else x1
            dst = yout if l == depth - 1 else x1
            R0 = slot * S
            vb = l * NV
            chan_phase()
            A.reset(base_mark)
            cqnT = A.alloc(2 * S * 2).rearrange("p (c t) -> p c t", c=2)
            ckvnT = A.alloc(S * 2)
            kropeT = A.alloc(S * 2)
            pA_mark = A.mark()
            ufour = A.alloc(NS * 256 * 2).rearrange("p (s c) -> p s c", c=256)
            p1_mark = A.mark()
            wA = A.alloc(8 * 448 * 2).rearrange("p (k n) -> p k n", k=8)
            wF = A.alloc(8 * 256 * 2).rearrange("p (k n) -> p k n", k=8)
            wB = A.alloc(6 * 8 * 128 * 2).rearrange("p (c k n) -> p c k n", c=6, k=8)
            xb2 = [A.alloc(D * 4, F32) for _ in range(2)]
            xn = [A.alloc(D * 2) for _ in range(4)]
            xb = [xb2[i % 2] for i in range(4)]
            xkeys = [("xb", i % 2) for i in range(4)]
            nkeys = [("xn", i) for i in range(4)]
            hTs = [A.alloc(8 * TT * 2).rearrange("p (k t) -> p k t", k=8) for _ in range(2)]
            tnh = [A.alloc(TT * 4, F32) for _ in range(4)]
            stg = [A.alloc(2 * TT * 2).rearrange("p (c t) -> p c t", c=2) for _ in range(2)]
            tokst = [A.alloc(512 * 2) for _ in range(2)]
            rtmp = [A.alloc(64 * 4, F32) for _ in range(2)]
            pstb = [(bank[6].bitcast(BF16), ("bk", 6)), (bank[7].bitcast(BF16), ("bk", 7))]
            cw = newchan()
            cx = [newchan() for _ in range(2)]
            cx = [cx[i % 2] for i in range(4)]
            cst = [newchan() for _ in range(2)]
            dma(wA, wview(l, "WA", "(p k n) -> p k n", p=128, k=8), cw, ["wbf"], ["wA"])
            dma(wF, wview(l, "WF", "(p k n) -> p k n", p=128, k=8), cw, ["wbf"], ["wF"])
            dma(wB, wview(l, "WB", "(c p k n) -> p c k n", c=6, p=128, k=8), cw, ["wbf"], ["wB"])
            dma(gpost, gpost_d[l:l + 1, :].partition_broadcast(128), cw, [], ["gpost"])

            for tt_i in range(NT):
                hT = hTs[tt_i % 2]
                hk = ("hT", id(hT))
                make_hT_a(l, src, R0 + tt_i * TT, xb, xn, cx, xkeys, nkeys, 0)
                make_hT_b(l, xn, nkeys, hT, pstb)
                t0 = tt_i * TT
                def fm_pool(c, tt_i=tt_i, t0=t0, hT=hT, hk=hk):
                    st = stg[0]
                    ps = bank[3]
                    for k in range(8):
                        mm(ps, wB[:, c, k, :], hT[:, k, :], k == 0, k == 7, [hk, "wB"], [("bk", 3)])
                    cp("act", st[:, c, :], ps, [("bk", 3)], [("stg", 0)])
                    if c == 1:
                        dma(upool_d[:, :, t0:t0 + TT].rearrange("c p t -> p c t"), st, cst[0], [("stg", 0)],
                            [("upool_d", tt_i)])

                def fm_glu(c, tt_i=tt_i, t0=t0, hT=hT, hk=hk):
                    st = stg[1]
                    psb_, psa_ = bank[2], bank[3]
                    for k in range(8):
                        mm(psb_, wB[:, 4 + c, k, :], hT[:, k, :], k == 0, k == 7, [hk, "wB"], [("bk", 2)])
                    act(tnh[c], psb_, AF.Tanh, [("bk", 2)], [("tnh", c)], scale=0.5)
                    for k in range(8):
                        mm(psa_, wB[:, 2 + c, k, :], hT[:, k, :], k == 0, k == 7, [hk, "wB"], [("bk", 3)])
                    stt(st[:, c, :], tnh[c], 1.0, psa_, ALU.add, ALU.mult, [("tnh", c), ("bk", 3)], [("stg", 1)])
                    if c == 1:
                        dma(aconv_d[:, :, t0:t0 + TT].rearrange("c p t -> p c t"), st, cst[1], [("stg", 1)],
                            [("aconv_d", tt_i)])
                fm = [lambda: fm_pool(0), lambda: fm_pool(1), lambda: fm_glu(0), lambda: fm_glu(1)]
                for sub in range(4):
                    si = tt_i * 4 + sub
                    psA, psF = bank[sub % 2], bank[4 + sub % 2]
                    kA, kF = ("bk", sub % 2), ("bk", 4 + sub % 2)
                    tk = tokst[sub % 2]
                    for k in range(8):
                        mm(psA[:, 0:448], hT[:, k, sub * 128:(sub + 1) * 128], wA[:, k, :], k == 0, k == 7,
                           [hk, "wA"], [kA])
                    for k in range(8):
                        mm(psF[:, 0:256], hT[:, k, sub * 128:(sub + 1) * 128], wF[:, k, :], k == 0, k == 7,
                           [hk, "wF"], [kF])
                    cp("act", ufour[:, si, :], psF[:, 0:256], [kF], [("ufour", si)])
                    sq = small[:, 40 + 2 * (sub % 2):42 + 2 * (sub % 2)]
                    junk = tnh[0][:, 0:256]
                    act(junk, psA[:, 0:256], AF.Square, [kA], [("tnh", 0), ("sq", sub % 2)],
                        accum=sq[:, 0:1])
                    act(junk[:, 0:128], psA[:, 256:384], AF.Square, [kA], [("tnh", 0), ("sq", sub % 2)],
                        accum=sq[:, 1:2])
                    rs = small[:, 48 + 2 * (sub % 2):50 + 2 * (sub % 2)]
                    ts("pool", rs[:, 0:1], sq[:, 0:1], 1.0 / 256, ALU.mult, [("sq", sub % 2)], [("rs", sub % 2)],
                       s2=1e-6, op1=ALU.add)
                    ts("pool", rs[:, 1:2], sq[:, 1:2], 1.0 / 128, ALU.mult, [("sq", sub % 2)], [("rs", sub % 2)],
                       s2=1e-6, op1=ALU.add)
                    rq = small[:, 56 + 2 * (sub % 2):58 + 2 * (sub % 2)]
                    tt("pool", rq, rs, mhalf[:, 0:2], ALU.pow, [("rs", sub % 2), "mhalf"], [("rq", sub % 2)])
                    act(tk[:, 0:256], psA[:, 0:256], AF.Copy, [kA, ("rq", sub % 2)], [("tk", sub % 2)],
                        scale=rq[:, 0:1])
                    act(tk[:, 256:384], psA[:, 256:384], AF.Copy, [kA, ("rq", sub % 2)], [("tk", sub % 2)],
                        scale=rq[:, 1:2])
                    rt = rtmp[sub % 2]
                    cosk = ropetok[:, si * 32:(si + 1) * 32]
                    sink = ropetok[:, (NS + si) * 32:(NS + si + 1) * 32]
                    tt("dve", rt[:, 0:32], psA[:, 416:448], sink, ALU.mult, [kA, "ropetok"], [("rt", sub % 2)])
                    tt("dve", rt[:, 32:64], psA[:, 384:416], cosk, ALU.mult, [kA, "ropetok"],
                       [("rt2", sub % 2)])
                    tt("dve", tk[:, 384:416], rt[:, 0:32], rt[:, 32:64], ALU.add, [("rt", sub % 2), ("rt2", sub % 2)],
                       [("tk", sub % 2)])
                    fm[sub]()
                    pt, ptk = pstb[sub % 2]
                    for j in range(3):
                        tr(pt[:, j * 128:(j + 1) * 128], tk[:, j * 128:(j + 1) * 128], [("tk", sub % 2)],
                           [ptk])
                    tr(pt[0:32, 384:512], tk[:, 384:416], [("tk", sub % 2)], [ptk])
                    tsl = slice(si * 128, (si + 1) * 128)
                    tt("dve", cqnT[:, :, tsl], pt[:, 0:256].rearrange("p (c t) -> p c t", c=2),
                       vec[:, vb + V_GQ:vb + V_GQ + 2].unsqueeze(2).broadcast_to([128, 2, 128]), ALU.mult,
                       [ptk, "vec"], [("cqnT", si)])
                    ts("dve", ckvnT[:, tsl], pt[:, 256:384], vec[:, vb + V_GKV:vb + V_GKV + 1], ALU.mult,
                       [ptk, "vec"], [("ckvnT", si)])
                    cp("dve", kropeT[0:32, tsl], pt[0:32, 384:512], [ptk], [("kropeT", si)])
            P.barrier()
            chan_phase()
            A.reset(p1_mark)
            tabs = [A.alloc(2 * 4 * TT * 2).rearrange("p (a i t) -> p a i t", a=2, i=4) for _ in range(3)]
            fev = [A.alloc(TT * 2) for _ in range(4)]
            gev = [A.alloc(TT * 2) for _ in range(2)]
            ofst = A.alloc(2 * TT * 2).rearrange("p (c t) -> p c t", c=2)
            wfo = A.alloc(2 * 256 * 2).rearrange("p (k n) -> p k n", k=2)
            ctab = [newchan() for _ in range(6)]
            cof = newchan()
            dma(wfo, wview(l, "FOURW", "(p k n) -> p k n", p=128, k=2), cof, ["wbf"], ["wfo"])
            U = A.alloc(2 * 528 * 2).rearrange("p (c t) -> p c t", c=2)
            p1 = A.alloc(2 * 528 * 4, F32).rearrange("p (c t) -> p c t", c=2)
            p2 = A.alloc(2 * 528 * 4, F32).rearrange("p (c t) -> p c t", c=2)
            p3 = A.alloc(528 * 4, F32)
            p4 = A.alloc(528 * 4, F32)
            pinvE = A.alloc(2 * 16 * 4, F32).rearrange("p (c t) -> p c t", c=2)
            pinvM = vec[:, vb + V_PINV:vb + V_PINV + 2]
            acc = A.alloc(2 * TT * 4, F32).rearrange("p (c t) -> p c t", c=2)
            pdf = A.alloc(2 * TT * 2).rearrange("p (c t) -> p c t", c=2)
            Ac = A.alloc(2 * 544 * 2).rearrange("p (c t) -> p c t", c=2)
            zz = A.alloc(2 * TT * 4, F32).rearrange("p (c t) -> p c t", c=2)
            dd = A.alloc(2 * TT * 4, F32).rearrange("p (c t) -> p c t", c=2)
            ybf = A.alloc(2 * TT * 2).rearrange("p (c t) -> p c t", c=2)
            ysq = A.alloc(2 * TT * 2).rearrange("p (c t) -> p c t", c=2)
            mean = A.alloc(TT * 4, F32)
            var = A.alloc(TT * 4, F32)
            rstd = A.alloc(TT * 4, F32)
            sbf = A.alloc(2 * TT * 2).rearrange("p (c t) -> p c t", c=2)
            opst = A.alloc(4 * TT * 2).rearrange("p (c t) -> p c t", c=4)
            wpool = A.alloc(2 * 128 * 2).rearrange("p (c n) -> p c n", c=2)
            wcpw = A.alloc(2 * 256 * 2).rearrange("p (k n) -> p k n", k=2)
            tnhb = [A.alloc(TT * 4, F32) for _ in range(2)]
            cbi = [newchan() for _ in range(2)]
            cbo = [newchan() for _ in range(2)]
            dma(wpool, wview(l, "POOLW", "(p c n) -> p c n", p=128, c=2), cof, ["wbf"], ["wpool"])
            dma(wcpw, wview(l, "CONVPW", "(p k n) -> p k n", p=128, k=2), cof, ["wbf"], ["wcpw"])
            dma(pinvE[:, :, 0:8], pinv_d[:, :, 0:8], cof, [], ["pinvE"])
            dma(pinvE[:, :, 8:16], pinv_d[:, :, S - 8:S], cof, [], ["pinvE"])
            dg = A.alloc(62 * 128 * 2).rearrange("p (i n) -> p i n", i=62)
            for i_ in range(62):
                ts("dve", dg[:, i_, :], ident, vder[:, vb + V_CW + i_:vb + V_CW + i_ + 1], ALU.mult,
                   ["ident", "vder"], ["dg"])
            def branch_pc(tb):
                t0 = tb * TT
                lo, hi = max(t0 - 8, 0), min(t0 + TT + 8, S)
                if t0 == 0:
                    memset("pool", U[:, :, 0:8], 0.0, ["U"])
                if t0 + TT == S:
                    memset("pool", U[:, :, 520:528], 0.0, ["U"])
                dma(U[:, :, lo - (t0 - 8):hi - (t0 - 8)], upool_d[:, :, lo:hi].rearrange("c p t -> p c t"), cbi[0],
                    [("upool_d", i) for i in range(NT)], ["U"])
                lo, hi = max(t0 - 15, 0), min(t0 + TT + 15, S)
                if t0 == 0:
                    memset("pool", Ac[:, :, 0:15], 0.0, ["Ac"])
                if t0 + TT == S:
                    memset("pool", Ac[:, :, 527:544], 0.0, ["Ac"])
                dma(Ac[:, :, lo - (t0 - 15):hi - (t0 - 15)], aconv_d[:, :, lo:hi].rearrange("c p t -> p c t"), cbi[1],
                    [("aconv_d", i) for i in range(NT)], ["Ac"])
                tt("pool", p1[:, :, 0:527], U[:, :, 0:527], U[:, :, 1:528], ALU.add, ["U"], ["p1"])
                tt("pool", p2[:, :, 0:525], p1[:, :, 0:525], p1[:, :, 2:527], ALU.add, ["p1"], ["p2"])
                tt("pool", p3[:, 0:521], p2[:, 1, 0:521], p2[:, 1, 4:525], ALU.add, ["p2"], ["p3"])
                tt("pool", p4[:, 0:513], p3[:, 0:513], p3[:, 8:521], ALU.add, ["p3"], ["p4"])
                srcs = ((0, 0, p1[0:64, 0, 7:519]), (0, 64, p2[64:128, 0, 6:518]), (1, 0, p3[0:64, 4:516]),
                        (1, 64, p4[64:128, 0:512]))
                for (c, pb_, sap) in srcs:
                    ts("pool", acc[pb_:pb_ + 64, c, :], sap, pinvM[pb_:pb_ + 64, c:c + 1], ALU.mult,
                       ["p1", "p2", "p3", "p4", "vec"], [("accp", c, pb_)], s2=1.0, op1=ALU.mult)
                    if t0 == 0:
                        tt("pool", acc[pb_:pb_ + 64, c, 0:8], sap[:, 0:8], pinvE[pb_:pb_ + 64, c, 0:8], ALU.mult,
                           ["p1", "p2", "p3", "p4", "pinvE"], [("accp", c, pb_)])
                    if t0 + TT == S:
                        tt("pool", acc[pb_:pb_ + 64, c, 504:512], sap[:, 504:512], pinvE[pb_:pb_ + 64, c, 8:16],
                           ALU.mult, ["p1", "p2", "p3", "p4", "pinvE"], [("accp", c, pb_)])
                tt("pool", pdf, acc, U[:, :, 8:520], ALU.subtract, [("accp", c, pb_) for (c, pb_, _) in srcs] + ["U"],
                   ["pdf"])
                yield
                for c in range(2):
                    mm(bank[6], wpool[:, c, :], pdf[:, c, :], True, True, ["pdf", "wpool"], [("bk", 6)])
                    ts("dve", opst[:, c, :], bank[6], vec[:, vb + V_PSC + c:vb + V_PSC + c + 1], ALU.mult,
                       [("bk", 6), "vec"], [("opst", 0)])
                dma(obr_d[0:2, :, t0:t0 + TT].rearrange("c p t -> p c t"), opst[:, 0:2, :], cbo[0], [("opst", 0)],
                    [("obr_pool", tb)], eng="act")
                for c in range(2):
                    for k in range(CONV_K):
                        mm(bank[6], dg[:, c * 31 + k, :], Ac[:, c, k:k + TT], k == 0, k == CONV_K - 1, ["Ac", "dg"],
                           [("bk", 6)])
                    ts("dve", zz[:, c, :], bank[6], vec[:, vb + V_CB + c:vb + V_CB + c + 1], ALU.add,
                       [("bk", 6), "vec"], [("zz", c)])
                for c in range(2):
                    cp("act", ybf[:, c, :], zz[:, c, :], [("zz", c)], [("ybf", c)])
                    act(ysq[:, c, :], zz[:, c, :], AF.Square, [("zz", c)], [("ysq", c)])
                yield
                for c in range(2):
                    mm(bank[6], ones, ybf[:, c, :], c == 0, c == 1, [("ybf", c), "ones"], [("bk", 6)])
                act(mean, bank[6], AF.Copy, [("bk", 6)], ["mean"], scale=1.0 / 256)
                for c in range(2):
                    mm(bank[6], ones, ysq[:, c, :], c == 0, c == 1, [("ysq", c), "ones"], [("bk", 6)])
                tt("pool", rstd, mean, mean, ALU.mult, ["mean"], ["rstd"])
                stt(var, bank[6], 1.0 / 256, rstd, ALU.mult, ALU.subtract, [("bk", 6), "rstd"], ["var"])
                act(var, var, AF.Ln, ["var", "epsc"], ["var"], bias=epsc[:, 0:1])
                act(rstd, var, AF.Exp, ["var"], ["rstd"], scale=-0.5)
                for c in range(2):
                    tt("dve", dd[:, c, :], zz[:, c, :], mean, ALU.subtract, [("zz", c), "mean"], [("dd", c)])
                    tt("dve", dd[:, c, :], dd[:, c, :], rstd, ALU.mult, [("dd", c), "rstd"], [("dd", c)])
                    act(tnhb[c], dd[:, c, :], AF.Tanh, [("dd", c), "vder"], [("tnhb", c)],
                        scale=vder[:, vb + V_LNG + c:vb + V_LNG + c + 1],
                        bias=vder[:, vb + V_LNB + c:vb + V_LNB + c + 1])
                    ts("dve", zz[:, c, :], dd[:, c, :], vec[:, vb + V_LNG + c:vb + V_LNG + c + 1], ALU.mult,
                       [("dd", c), "vec"], [("zz", c)], s2=vec[:, vb + V_LNB + c:vb + V_LNB + c + 1], op1=ALU.add)
                    stt(sbf[:, c, :], tnhb[c], 1.0, zz[:, c, :], ALU.add, ALU.mult, [("tnhb", c), ("zz", c)],
                        [("sbf", c)])
                yield
                for co in range(2):
                    for k in range(2):
                        mm(bank[6], wcpw[:, k, co * 128:(co + 1) * 128], sbf[:, k, :], k == 0, k == 1,
                           [("sbf", k), "wcpw"], [("bk", 6)])
                    ts("dve", opst[:, 2 + co, :], bank[6], 0.5, ALU.mult, [("bk", 6), "vec"], [("opst", 1)],
                       s2=vec[:, vb + V_PWB + co:vb + V_PWB + co + 1], op1=ALU.add)
                dma(obr_d[6:8, :, t0:t0 + TT].rearrange("c p t -> p c t"), opst[:, 2:4, :], cbo[1], [("opst", 1)],
                    [("obr_conv", tb)], eng="act")

            gi = 0
            bgens = []
            dpend = [None]
            for j in range(NT):
                bgens.append(branch_pc(j))
                ngrp = NS // 4
                for ig in range(ngrp):
                    for st_ in range(4):
                        if ig == min(ngrp - 1, 2 * st_ + 1):
                            t_ = j - 3 + st_
                            if 0 <= t_ < NT:
                                next(bgens[t_], None)
                    tb = tabs[gi % 3]
                    for a in range(2):
                        dma(tb[:, a].rearrange("p i t -> p (i t)"), dft_d[a, j, ig], ctab[(gi % 3) * 2 + a], [],
                            [("tab", gi % 3, a)])
                    for ii in range(4):
                        i = ig * 4 + ii
                        for c in range(2):
                            for a in range(2):
                                mm(bank[a * 2 + c], ufour[:, i, c * 128:(c + 1) * 128], tb[:, a, ii, :],
                                   i == 0, i == NS - 1, [("tab", gi % 3, a)], [("bk", a * 2 + c)])
                    gi += 1
                    if ig == 1 and dpend[0] is not None:
                        dpend[0]()
                        dpend[0] = None
                for a in range(2):
                    for c in range(2):
                        act(fev[a * 2 + c], bank[a * 2 + c], AF.Copy, [("bk", a * 2 + c)], [("fev", a * 2 + c)],
                            scale=float((S * 64) ** -0.5))

                def post_j(j=j):
                    for c in range(2):
                        mm(bank[4 + c], c64, fev[c], True, False, [("fev", c), "cbd"], [("bk", 4 + c)])
                        mm(bank[4 + c], ns64, fev[2 + c], False, True, [("fev", 2 + c), "cbd"], [("bk", 4 + c)])
                        cp("dve", gev[c], bank[4 + c], [("bk", 4 + c)], [("gev", c)])
                    for co in range(2):
                        for k in range(2):
                            mm(bank[7], wfo[:, k, co * 128:(co + 1) * 128], gev[k], k == 0, k == 1,
                               [("gev", k), "wfo"], [("bk", 7)])
                        cp("dve", ofst[:, co, :], bank[7], [("bk", 7)], [("ofst", 0)])
                    dma(obr_d[8:10, :, j * TT:(j + 1) * TT].rearrange("c p t -> p c t"), ofst, cof, [("ofst", 0)],
                        [("obr_four", j)], eng="act")
                dpend[0] = post_j
            if dpend[0] is not None:
                dpend[0]()
                dpend[0] = None
            for i_ in range(NT, NT + 3):
                for st_ in range(4):
                    t_ = i_ - 3 + st_
                    if 0 <= t_ < NT:
                        next(bgens[t_], None)
            P.barrier()
            chan_phase()
            A.reset(pA_mark)
            wuq = A.alloc(2 * 768 * 2).rearrange("p (k n) -> p k n", k=2)
            wuqs = A.alloc(2 * 768 * 2).rearrange("p (k n) -> p k n", k=2)
            wukv = A.alloc(1024 * 2)
            KT = [A.alloc(S * 2) for _ in range(2)]
            VT = [A.alloc(NS * 128 * 2).rearrange("p (s c) -> p s c", c=128) for _ in range(2)]
            QT = [A.alloc(S * 2) for _ in range(2)]
            pT = [A.alloc(TT * 2) for _ in range(4)]
            rtab = [A.alloc(2 * TT * 4, F32).rearrange("p (a t) -> p a t", a=2) for _ in range(2)]
            rr1 = [A.alloc(TT * 4, F32) for _ in range(2)]
            rr2 = [A.alloc(TT * 4, F32) for _ in range(2)]
            rec = [A.alloc(TT * 4, F32) for _ in range(2)]
            oat = [A.alloc(TT * 2) for _ in range(2)]
            cwa = newchan()
            crt = [newchan() for _ in range(2)]
            coa = [newchan() for _ in range(2)]
            dma(wuq, wview(l, "WUQ", "(p k n) -> p k n", p=128, k=2), cwa, ["wbf"], ["wuq"])
            dma(wuqs, wview(l, "WUQS", "(p k n) -> p k n", p=128, k=2), cwa, ["wbf"], ["wuqs"])
            dma(wukv, wview(l, "WUKV", "(p n) -> p n", p=128), cwa, ["wbf"], ["wukv"])
            memset("pool", VT[0][:, :, 64:128], 1.0, [("VT", 0)])
            memset("pool", VT[1][:, :, 0:64], 1.0, [("VT", 1)])
            for hb in range(2):
                for j in range(NT):
                    cp("dve", KT[hb][64:96, j * TT:(j + 1) * TT], kropeT[0:32, j * TT:(j + 1) * TT], [],
                       [("KT", hb)])
            rti = [0]

            def build_head(h):
                hb = h % 2
                for j in range(NT):
                    mm(bank[5][0:64, :], wukv[:, h * 128:h * 128 + 64], ckvnT[:, j * TT:(j + 1) * TT], True, True,
                       ["wukv"], [("bk", 5)])
                    cp("dve", KT[hb][0:64, j * TT:(j + 1) * TT], bank[5][0:64, :], [("bk", 5)], [("KT", hb)])
                for g in range(NS // 8):
                    for ii in range(8):
                        i = g * 8 + ii
                        mm(bank[6][:, ii * 64:(ii + 1) * 64], ckvnT[:, i * 128:(i + 1) * 128],
                           wukv[:, h * 128 + 64:h * 128 + 128], True, True, ["wukv"], [("bk", 6)])
                    vo = 0 if hb == 0 else 64
                    cp("dve", VT[hb][:, g * 8:(g + 1) * 8, vo:vo + 64],
                       bank[6].rearrange("p (i c) -> p i c", i=8), [("bk", 6)], [("VT", hb)])
                for j in range(NT):
                    rb = rti[0] % 2
                    rti[0] += 1
                    dma(rtab[rb][64:96, :, :], ropeT_d[:, :, j * TT:(j + 1) * TT].rearrange("a d t -> d a t"),
                        crt[rb], [], [("rtab", rb)])
                    psA, psB = bank[5], bank[7]
                    for k in range(2):
                        mm(psA[0:96, :], wuq[:, k, h * 96:(h + 1) * 96], cqnT[:, k, j * TT:(j + 1) * TT], k == 0,
                           k == 1, ["wuq"], [("bk", 5)])
                    for k in range(2):
                        mm(psB[0:96, :], wuqs[:, k, h * 96:(h + 1) * 96], cqnT[:, k, j * TT:(j + 1) * TT], k == 0,
                           k == 1, ["wuqs"], [("bk", 7)])
                    qs = slice(j * TT, (j + 1) * TT)
                    cp("dve", QT[hb][0:64, qs], psA[0:64, :], [("bk", 5)], [("QT", hb)])
                    tt("dve", rr1[rb][64:96, :], psB[64:96, :], rtab[rb][64:96, 1, :], ALU.mult,
                       [("bk", 7), ("rtab", rb)], [("rr1", rb)])
                    tt("dve", rr2[rb][64:96, :], psA[64:96, :], rtab[rb][64:96, 0, :], ALU.mult,
                       [("bk", 5), ("rtab", rb)], [("rr2", rb)])
                    tt("dve", QT[hb][64:96, qs], rr1[rb][64:96, :], rr2[rb][64:96, :], ALU.add,
                       [("rr1", rb), ("rr2", rb)], [("QT", hb)])

            build_head(0)
            qi = 0
            pi = 0
            for h in range(8):
                hb = h % 2
                tot = NT * NS

                def score(s_, hb=hb):
                    j_, kc_ = divmod(s_, NS)
                    sb = s_ % 3
                    mm(bank[sb], KT[hb][0:96, kc_ * 128:(kc_ + 1) * 128], QT[hb][0:96, j_ * TT:(j_ + 1) * TT], True,
                       True, [("KT", hb), ("QT", hb)], [("bk", sb)])
                score(0)
                score(1)
                for s_ in range(tot):
                    j, kc = divmod(s_, NS)
                    ob = 3 + (qi + j) % 2
                    ok = ("bk", ob)
                    qs = slice(j * TT, (j + 1) * TT)
                    if s_ + 2 < tot:
                        score(s_ + 2)
                    pb = pi % 4
                    pi += 1
                    act(pT[pb], bank[s_ % 3], AF.Exp, [("bk", s_ % 3)], [("pT", pb)], scale=SCALE)
                    mm(bank[ob], VT[hb][:, kc, :], pT[pb], kc == 0, kc == NS - 1, [("VT", hb), ("pT", pb)], [ok])
                    if kc == NS - 1:
                        d0, s0 = (0, 64) if hb == 0 else (64, 0)
                        rb = (qi + j) % 2
                        P.op("dve", (lambda o_, i_: (lambda e: e.reciprocal(out=o_, in_=i_)))(
                            rec[rb][d0:d0 + 64, :], bank[ob][s0:s0 + 64, :]), r=[ok], w=[("rec", rb)])
                        tt("dve", oat[rb][d0:d0 + 64, :], bank[ob][d0:d0 + 64, :], rec[rb][d0:d0 + 64, :], ALU.mult,
                           [ok, ("rec", rb)], [("oat", rb)])
                        dma(obr_d[2 + h // 2, d0:d0 + 64, qs], oat[rb][d0:d0 + 64, :], coa[rb], [("oat", rb)],
                            [("obr_attn", h // 2, j, hb)])
                        if j == NT // 2 - 1 and h + 1 < 8:
                            build_head(h + 1)
                qi += NT
            P.barrier()
            chan_phase()
            A.reset(base_mark)
            wout = A.alloc(8 * 1024 * 2).rearrange("p (k n) -> p k n", k=8)
            xbs = [[A.alloc(D * 4, F32) for _ in range(4)] for _ in range(2)]
            xns = [A.alloc(D * 2) for _ in range(4)]
            hTs = [A.alloc(8 * TT * 2).rearrange("p (k t) -> p k t", k=8) for _ in range(2)]
            Abufs = [A.alloc(10 * TT * 2).rearrange("p (c t) -> p c t", c=10) for _ in range(2)]
            oin = A.alloc(10 * TT * 2).rearrange("p (c t) -> p c t", c=10)
            mT = A.alloc(8 * TT * 2).rearrange("p (k t) -> p k t", k=8)
            wring = [A.alloc(4 * 1024 * 2).rearrange("p (c k n) -> p c k n", c=4, k=8) for _ in range(3)]
            wupr = [A.alloc(10 * 128 * 2).rearrange("p (k n) -> p k n", k=10) for _ in range(2)]
            tnh = [A.alloc(TT * 4, F32) for _ in range(4)]
            tmp = [A.alloc(TT * 4, F32) for _ in range(4)]
            s01 = A.alloc(TT * 4, F32)
            s23 = A.alloc(TT * 4, F32)
            otmp = [A.alloc(D * 4, F32) for _ in range(2)]
            pstb = [(bank[6].bitcast(BF16), ("bk", 6)), (bank[7].bitcast(BF16), ("bk", 7))]
            cw2 = newchan()
            cxs = [[newchan() for _ in range(4)] for _ in range(2)]
            cwr = [newchan() for _ in range(3)]
            cwu = [newchan() for _ in range(2)]
            cin = newchan()
            cout = [newchan() for _ in range(2)]
            dma(wout, wview(l, "WOUT", "(p k n) -> p k n", p=128, k=8), cw2, ["wbf"], ["wout"])
            WGv = wview(l, "WG", "(c p k n) -> p c k n", c=10, p=128, k=8)
            WMv = wview(l, "WM", "(j i p k n) -> j p i k n", j=8, i=4, p=128, k=8)
            WUPv = wview(l, "WUP", "(j p k n) -> j p k n", j=8, p=128, k=10)
            st2 = dict(wri=0, gbank=0, oi=0)
            nkeys = [("xn", i) for i in range(4)]

            def prep_a(tt_i):
                pb_ = tt_i % 2
                make_hT_a(l, src, R0 + tt_i * TT, xbs[pb_], xns, cxs[pb_], [("xb", pb_, i) for i in range(4)], nkeys,
                          16 * pb_)

            def prep_b(tt_i):
                pb_ = tt_i % 2
                t0 = tt_i * TT
                hT = hTs[pb_]
                hk = ("hT", id(hT))
                Abuf = Abufs[pb_]
                make_hT_b(l, xns, nkeys, hT, pstb)
                dma(oin, obr_d[:, :, t0:t0 + TT].rearrange("c p t -> p c t"), cin, [], ["oin"])
                for g3, (c0, c1) in enumerate(((0, 4), (4, 8), (8, 10))):
                    wri = st2["wri"]
                    wr = wring[wri % 3]
                    wk = ("wring", wri % 3)
                    dma(wr[:, 0:c1 - c0], WGv[:, c0:c1], cwr[wri % 3], ["wbf"], [wk])
                    for c in range(c0, c1):
                        gb = st2["gbank"]
                        ps = bank[gb % 2]
                        pk = ("bk", gb % 2)
                        tb = gb % 4
                        st2["gbank"] += 1
                        for k in range(8):
                            mm(ps, wr[:, c - c0, k, :], hT[:, k, :], k == 0, k == 7, [hk, wk], [pk])
                        act(tnh[tb], ps, AF.Tanh, [pk], [("tnh", tb)], scale=0.5)
                        stt(Abuf[:, c, :], tnh[tb], 1.0, ps, ALU.add, ALU.mult, [("tnh", tb), pk], [("A", pb_, c)])
                        stt(Abuf[:, c, :], oin[:, c, :], 0.25, Abuf[:, c, :], ALU.mult, ALU.mult,
                            ["oin", ("A", pb_, c)], [("A", pb_, c)])
                    st2["wri"] += 1

            def merge(tt_i, mid=None):
                pb_ = tt_i % 2
                t0 = tt_i * TT
                hT = hTs[pb_]
                hk = ("hT", id(hT))
                Abuf = Abufs[pb_]
                xb = xbs[pb_]
                kk0 = (0, 2, 6, 8)
                nk = (2, 4, 2, 2)
                for j in range(8):
                    wri = st2["wri"]
                    wr = wring[wri % 3]
                    wk = ("wring", wri % 3)
                    wu = wupr[j % 2]
                    uk = ("wupr", j % 2)
                    if j == 0 and st2.get("pref") == wri:
                        pass
                    else:
                        dma(wr, WMv[j], cwr[wri % 3], ["wbf"], [wk])
                        dma(wu, WUPv[j], cwu[j % 2], ["wbf"], [uk])
                    for i in range(4):
                        gb = st2["gbank"]
                        ps = bank[gb % 2]
                        pk = ("bk", gb % 2)
                        tb = gb % 4
                        st2["gbank"] += 1
                        for k in range(8):
                            mm(ps, wr[:, i, k, :], hT[:, k, :], k == 0, k == 7, [hk, wk], [pk])
                        act(tnh[tb], ps, AF.Tanh, [pk, "vder"], [("tnh", tb)], scale=0.5,
                            bias=vder[:, vb + V_GB + i * 8 + j:vb + V_GB + i * 8 + j + 1])
                        pu = bank[2 + i % 2]
                        pku = ("bk", 2 + i % 2)
                        for k in range(nk[i]):
                            mm(pu, wu[:, kk0[i] + k, :], Abuf[:, kk0[i] + k, :], k == 0, k == nk[i] - 1,
                               [uk, ("A", pb_, kk0[i] + k)], [pku])
                        stt(tmp[i], tnh[tb], 1.0, pu, ALU.add, ALU.mult, [("tnh", tb), pku], [("tmp", i)])
                    tt("pool", s01, tmp[0], tmp[1], ALU.add, [("tmp", 0), ("tmp", 1)], ["s01"])
                    tt("pool", s23, tmp[2], tmp[3], ALU.add, [("tmp", 2), ("tmp", 3)], ["s23"])
                    tt("pool", mT[:, j, :], s01, s23, ALU.add, ["s01", "s23"], [("mT", j)])
                    st2["wri"] += 1
                    if j == 3 and mid is not None:
                        mid()
                if tt_i + 1 < NT:
                    wri = st2["wri"]
                    dma(wring[wri % 3], WMv[0], cwr[wri % 3], ["wbf"], [("wring", wri % 3)])
                    dma(wupr[0], WUPv[0], cwu[0], ["wbf"], [("wupr", 0)])
                    st2["pref"] = wri
                for sub in range(4):
                    pp = 4 if sub % 2 == 0 else 2
                    po = psum_t[:, pp * 512:(pp + 2) * 512]
                    kx = ("xb", pb_, sub)
                    for n in range(2):
                        for k in range(8):
                            mm(bank[pp + n], mT[:, k, sub * 128:(sub + 1) * 128], wout[:, k, n * 512:(n + 1) * 512],
                               k == 0, k == 7, [("mT", k), "wout"], [("bk", pp + n)])
                    so = 32 + 8 * (sub % 2)
                    sq = small[:, so:so + 2]
                    for n in range(2):
                        act(tnh[n][:, 0:512], bank[pp + n], AF.Square, [("bk", pp + n)], [("tnh", n), ("osq", sub % 2, n)],
                            accum=sq[:, n:n + 1])
                    tt("pool", small[:, so + 2:so + 3], sq[:, 0:1], sq[:, 1:2], ALU.add,
                       [("osq", sub % 2, 0), ("osq", sub % 2, 1)], [("os1", sub % 2)])
                    ts("pool", small[:, so + 3:so + 4], small[:, so + 2:so + 3], 1.0 / D, ALU.mult, [("os1", sub % 2)],
                       [("os2", sub % 2)], s2=1e-6, op1=ALU.add)
                    tt("pool", small[:, so + 4:so + 5], small[:, so + 3:so + 4], mhalf[:, 0:1], ALU.pow,
                       [("os2", sub % 2), "mhalf"], [("os3", sub % 2)])
                    ot = otmp[sub % 2]
                    stt(ot, po, small[:, so + 4:so + 5], gpost, ALU.mult, ALU.mult,
                        [("bk", pp), ("bk", pp + 1), ("os3", sub % 2), "gpost"], [("otmp", sub % 2)])
                    tt("dve", xb[sub], ot, xb[sub], ALU.add, [("otmp", sub % 2), kx], [kx])
                    dma(dst[R0 + t0 + sub * 128:R0 + t0 + (sub + 1) * 128, :], xb[sub], cout[sub % 2], [kx],
                        [("dst", slot, l)])

            prep_a(0)
            prep_b(0)
            for tt_i in range(NT):
                if tt_i + 1 < NT:
                    prep_a(tt_i + 1)
                    merge(tt_i, (lambda t=tt_i + 1: prep_b(t)))
                else:
                    merge(tt_i)
            P.barrier()

    P.finalize()
    with nc.Block() as block:
        @block.tensor
        def _(e):
            P.replay("pe", e, sems)

        @block.scalar
        def _(e):
            P.replay("act", e, sems)

        @block.vector
        def _(e):
            P.replay("dve", e, sems)

        @block.gpsimd
        def _(e):
            P.replay("pool", e, sems)

        @block.sync
        def _(e):
            P.replay("sp", e, sems)
    es.close()
    return nc


def _pack_layer(l, w):
    f = np.float32
    out = np.zeros(W_LAYER, f)

    def put(name, arr):
        a = np.ascontiguousarray(arr, dtype=f).reshape(-1)
        assert a.size == dict(W_SIZES)[name], (name, a.size)
        out[W_OFF[name]:W_OFF[name] + a.size] = a
    w_in = w["w_in"][l]
    wk = w_in.reshape(8, 128, NIN)
    colsA = np.concatenate([np.arange(OFF_CQ, OFF_CQ + 256), np.arange(OFF_KVA, OFF_KVA + 160),
                            np.arange(OFF_KVA + 144, OFF_KVA + 160), np.arange(OFF_KVA + 128, OFF_KVA + 144)])
    put("WA", wk[:, :, colsA].transpose(1, 0, 2))
    put("WF", wk[:, :, OFF_UFOUR:OFF_UFOUR + 256].transpose(1, 0, 2))
    colsB = np.concatenate([np.arange(OFF_UPOOL, OFF_UPOOL + 256), np.arange(OFF_UCONV, OFF_UCONV + 512)])
    put("WB", wk[:, :, colsB].reshape(8, 128, 6, 128).transpose(2, 1, 0, 3))
    put("WG", wk[:, :, OFF_GPOOL:OFF_GPOOL + 1280].reshape(8, 128, 10, 128).transpose(2, 1, 0, 3))
    put("WM", wk[:, :, OFF_MERGE:].reshape(8, 128, 4, 8, 128).transpose(3, 2, 1, 0, 4))
    up = np.concatenate([w["w_up_pool"][l], w["w_up_attn"][l], w["w_up_conv"][l], w["w_up_fourier"][l]], 0)
    put("WUP", up.reshape(10, 128, 8, 128).transpose(2, 1, 0, 3))
    put("WOUT", w["w_out"][l].reshape(8, 128, 1024).transpose(1, 0, 2))
    uq = w["w_uq"][l]
    put("WUQ", uq.reshape(2, 128, 768).transpose(1, 0, 2))
    uqs = uq.reshape(256, 8, 96).copy()
    uqs[:, :, 64:80], uqs[:, :, 80:96] = uq.reshape(256, 8, 96)[:, :, 80:96], uq.reshape(256, 8, 96)[:, :, 64:80]
    put("WUQS", uqs.reshape(2, 128, 768).transpose(1, 0, 2))
    put("WUKV", w["w_ukv"][l])
    pw = np.zeros((128, 2, 128), f)
    for c in range(2):
        pw[0:64, c, 0:64] = w["pool_w"][l][2 * c]
        pw[64:128, c, 64:128] = w["pool_w"][l][2 * c + 1]
    put("POOLW", pw)
    put("CONVPW", w["conv_pw_w"][l].reshape(2, 128, 256).transpose(1, 0, 2))
    put("FOURW", w["fourier_w"][l].reshape(2, 128, 256).transpose(1, 0, 2))
    return out


def _pack_vecs(l, w):
    v = np.zeros((128, NV), np.float32)
    v[:, V_GPRE:V_GPRE + 8] = w["pre_norm_g"][l].reshape(8, 128).T
    v[:, V_GQ:V_GQ + 2] = w["q_norm_g"][l].reshape(2, 128).T
    v[:, V_GKV] = w["kv_norm_g"][l]
    v[:, V_GB:V_GB + 32] = w["gate_b"][l].reshape(32, 128).T
    v[:, V_PSC:V_PSC + 2] = w["pool_scale"][l].reshape(2, 128).T
    v[:, V_CW:V_CW + 62] = w["conv_w"][l].reshape(31, 2, 128).transpose(2, 1, 0).reshape(128, 62)
    v[:, V_CB:V_CB + 2] = w["conv_b"][l].reshape(2, 128).T
    v[:, V_LNG:V_LNG + 2] = w["conv_ln_g"][l].reshape(2, 128).T
    v[:, V_LNB:V_LNB + 2] = w["conv_ln_b"][l].reshape(2, 128).T
    v[:, V_PWB:V_PWB + 2] = w["conv_pw_b"][l].reshape(2, 128).T
    for c in range(2):
        v[0:64, V_PINV + c] = 1.0 / (2, 4, 8, 16)[2 * c]
        v[64:128, V_PINV + c] = 1.0 / (2, 4, 8, 16)[2 * c + 1]
    return v


_CONST_CACHE = {}


def _consts(S):
    if S in _CONST_CACHE:
        return _CONST_CACHE[S]
    NS = S // 128
    f = np.float32
    inv_freq = (1.0 / (f(10000.0) ** (np.arange(0, 32, 2, dtype=f) / f(32)))).astype(f)
    ang = (np.arange(S, dtype=f)[:, None] * inv_freq[None, :]).astype(f)
    ang = np.concatenate([ang, ang], -1)
    cos = np.cos(ang).astype(f)
    sin = np.sin(ang).astype(f)
    sinS = sin.copy()
    sinS[:, 0:16] *= -1
    ropetok = np.concatenate([cos.reshape(NS, 128, 32).transpose(1, 0, 2).reshape(128, NS * 32),
                              sinS.reshape(NS, 128, 32).transpose(1, 0, 2).reshape(128, NS * 32)], 1)
    ropeT = np.stack([cos.T, sinS.T], 0).astype(f)
    idx = (np.arange(S, dtype=np.int64)[:, None] * np.arange(S, dtype=np.int64)[None, :]) % S
    angd = idx.astype(np.float64) * (2 * np.pi / S)
    dft = np.stack([np.cos(angd).astype(NPBF), np.sin(angd).astype(NPBF)], 0)
    dft = np.ascontiguousarray(dft.reshape(2, S // 512, 4, 128, S // TT, TT).transpose(0, 4, 1, 3, 2, 5)).reshape(
        2, S // TT, S // 512, 128, 4 * TT)
    a64 = (np.arange(64)[:, None] * np.arange(64)[None, :]) % 64 * (2 * np.pi / 64)
    cbd = np.zeros((128, 512), np.float64)
    for b in range(2):
        cbd[b * 64:(b + 1) * 64, b * 64:(b + 1) * 64] = np.cos(a64)
        cbd[b * 64:(b + 1) * 64, 128 + b * 64:128 + (b + 1) * 64] = -np.sin(a64)
    cbd[:, 256:384] = 1.0
    cbd[:, 384:512] = np.eye(128)
    t = np.arange(S)
    pinv = np.zeros((128, 2, S), f)
    for g, wdw in enumerate((2, 4, 8, 16)):
        lo = np.clip(t - wdw // 2, 0, S)
        hi = np.clip(t + wdw // 2, 0, S)
        pinv[(g % 2) * 64:(g % 2) * 64 + 64, g // 2, :] = (1.0 / (hi - lo).astype(f))[None, :]
    res = dict(ropetok=np.ascontiguousarray(ropetok, dtype=f), ropeT=ropeT, dft=dft, cbd=cbd.astype(NPBF), pinv=pinv)
    _CONST_CACHE[S] = res
    return res


_NC_CACHE = {}


def run(xseqs, w, S, ncores, nslot):
    key = (S, nslot)
    if key not in _NC_CACHE:
        _NC_CACHE[key] = build(S, nslot)
    nc = _NC_CACHE[key]
    wp = np.concatenate([_pack_layer(l, w) for l in range(DEPTH)])
    vv = np.concatenate([_pack_vecs(l, w) for l in range(DEPTH)], 1)
    cst = _consts(S)
    gpost = np.ascontiguousarray(w["post_norm_g"], dtype=np.float32)
    in_maps = []
    for c in range(ncores):
        m = dict(xin=np.ascontiguousarray(xseqs[c * nslot:(c + 1) * nslot].reshape(nslot * S, D)), wpack=wp, vecs=vv,
                 gpost=gpost, ropetok=cst["ropetok"], ropeT=cst["ropeT"], dft=cst["dft"], cbd=cst["cbd"],
                 pinv=cst["pinv"])
        in_maps.append(m)
    res = run_bass_kernel_spmd(nc, in_maps, core_ids=list(range(ncores)))
    return np.stack([np.asarray(r["yout"]).reshape(nslot, S, D) for r in res.results], 0).reshape(
        ncores * nslot, S, D)


def kernel(**inputs):
    w = {k: np.asarray(v, dtype=np.float32) for k, v in inputs.items() if k not in ("x_prompt", "x_sample")}
    xp = np.asarray(inputs["x_prompt"], dtype=np.float32)
    xs = np.asarray(inputs["x_sample"], dtype=np.float32)
    S = xp.shape[1]
    seqs = np.concatenate([xp, xs], 0)
    nseq = seqs.shape[0]
    nslot = 3
    order = list(range(nseq)) + [0] * (NCORES * nslot - nseq)
    xin = seqs[order]
    y = run(xin, w, S, NCORES, nslot)
    y = y[:nseq]
    return (np.ascontiguousarray(y[:xp.shape[0]]), np.ascontiguousarray(y[xp.shape[0]:]))
```
